# Optimizing a Trainium2 kernel written in Bass

```python
import math
import jax, jax.numpy as jnp
from jax import lax
import numpy as np

D_MODEL = 2048
BATCH = 16
SEQ = 2048
DEPTH = 4

N_META = 16
ATTN_WIDTH = 1024
POOL_WIDTH = D_MODEL - ATTN_WIDTH
DIFF_HEAD_DIM = 64
DIFF_V_DIM = 2 * DIFF_HEAD_DIM
N_DIFF_HEADS = ATTN_WIDTH // DIFF_V_DIM
POOL_WINDOWS = (2, 4, 8, 16)
N_POOL_GROUPS = len(POOL_WINDOWS)
POOL_GROUP_WIDTH = POOL_WIDTH // N_POOL_GROUPS
IN_WIDTH = 3 * ATTN_WIDTH + POOL_WIDTH
D_FF = 5632
ROPE_THETA = 10000.0
Q_BLOCK = 128
NORM_EPS = 1e-6
SUBLN_EPS = 1e-5

kernel_name = "hymba_diffattn_pool_macaron"


def rms_norm(x, g, eps):
    xf = x.astype(jnp.float32)
    y = xf * lax.rsqrt(jnp.mean(xf * xf, axis=-1, keepdims=True) + eps)
    return (y * g.astype(jnp.float32)).astype(x.dtype)


def swiglu(h, w_gate, w_up, w_down):
    return (jax.nn.silu(h @ w_gate) * (h @ w_up)) @ w_down


def rope_tables(length):
    pos = jnp.arange(length, dtype=jnp.float32)
    inv_freq = 1.0 / (ROPE_THETA ** (jnp.arange(0, DIFF_HEAD_DIM, 2, dtype=jnp.float32) / DIFF_HEAD_DIM))
    ang = pos[:, None] * inv_freq[None, :]
    ang = jnp.concatenate([ang, ang], axis=-1)
    return jnp.cos(ang), jnp.sin(ang)


def apply_rope(x, cos, sin):
    xf = x.astype(jnp.float32)
    half = DIFF_HEAD_DIM // 2
    rot = jnp.concatenate([-xf[..., half:], xf[..., :half]], axis=-1)
    c = cos[None, :, None, None, :]
    s = sin[None, :, None, None, :]
    return (xf * c + rot * s).astype(x.dtype)


def diff_attention(q, k, v, lam):
    length = q.shape[1]
    outs = []
    for i in range(length // Q_BLOCK):
        start, end = i * Q_BLOCK, (i + 1) * Q_BLOCK
        qb, kb, vb = q[:, start:end], k[:, :end], v[:, :end]
        s = jnp.einsum('bqhmd,bkhmd->bhmqk', qb, kb).astype(jnp.float32)
        qpos = start + jnp.arange(Q_BLOCK)
        kpos = jnp.arange(end)
        mask = kpos[None, :] <= qpos[:, None]
        s = jnp.where(mask, s, jnp.finfo(jnp.float32).min)
        p = jax.nn.softmax(s, axis=-1)
        a = p[:, :, 0] - lam * p[:, :, 1]
        outs.append(jnp.einsum('bhqk,bkhe->bqhe', a.astype(v.dtype), vb))
    return jnp.concatenate(outs, axis=1)


def causal_multiscale_pool(u, w_pool, pool_scale):
    b, length, _ = u.shape
    uf = u.reshape(b, length, N_POOL_GROUPS, POOL_GROUP_WIDTH).astype(jnp.float32)
    cs = jnp.cumsum(uf, axis=1)
    t = jnp.arange(length)
    means = []
    for g, w in enumerate(POOL_WINDOWS):
        c = cs[:, :, g]
        lagged = jnp.pad(c, ((0, 0), (w, 0), (0, 0)))[:, :length]
        count = jnp.minimum(t + 1, w).astype(jnp.float32)[None, :, None]
        means.append((c - lagged) / count)
    diff = (jnp.stack(means, axis=2) - uf).astype(u.dtype)
    y = jnp.einsum('blgc,gcd->blgd', diff, w_pool).reshape(b, length, POOL_WIDTH)
    return y * pool_scale


def hybrid_mixer(h, w_in, lam_q1, lam_k1, lam_q2, lam_k2, subln_g, w_pool, pool_scale, w_out, cos, sin, lam_init):
    b, length, _ = h.shape
    proj = h @ w_in
    q, k, v, u = jnp.split(proj, [ATTN_WIDTH, 2 * ATTN_WIDTH, 3 * ATTN_WIDTH], axis=-1)
    q = apply_rope(q.reshape(b, length, N_DIFF_HEADS, 2, DIFF_HEAD_DIM), cos, sin) * (DIFF_HEAD_DIM ** -0.5)
    k = apply_rope(k.reshape(b, length, N_DIFF_HEADS, 2, DIFF_HEAD_DIM), cos, sin)
    v = v.reshape(b, length, N_DIFF_HEADS, DIFF_V_DIM)
    lam = (jnp.exp(jnp.sum(lam_q1.astype(jnp.float32) * lam_k1.astype(jnp.float32)))
           - jnp.exp(jnp.sum(lam_q2.astype(jnp.float32) * lam_k2.astype(jnp.float32)))
           + lam_init)
    o = diff_attention(q, k, v, lam)
    o = rms_norm(o, subln_g, SUBLN_EPS) * (1.0 - lam_init)
    attn_out = o.reshape(b, length, ATTN_WIDTH)
    pool_out = causal_multiscale_pool(u, w_pool, pool_scale)
    return jnp.concatenate([attn_out, pool_out], axis=-1) @ w_out


def setup_inputs(seed: int = 0) -> dict:
    key = jax.random.key(seed)
    ks = jax.random.split(key, 24)
    f32 = jnp.float32
    nrm = lambda k, shape, s: jax.random.normal(k, shape, f32) * s
    gain = lambda k, shape: 1.0 + 0.02 * jax.random.normal(k, shape, f32)
    return {
        "x": nrm(ks[0], (BATCH, SEQ, D_MODEL), 1.0),
        "meta_tokens": nrm(ks[1], (N_META, D_MODEL), 1.0),
        "ffn1_norm_g": gain(ks[2], (DEPTH, D_MODEL)),
        "ffn1_w_gate": nrm(ks[3], (DEPTH, D_MODEL, D_FF), D_MODEL ** -0.5),
        "ffn1_w_up": nrm(ks[4], (DEPTH, D_MODEL, D_FF), D_MODEL ** -0.5),
        "ffn1_w_down": nrm(ks[5], (DEPTH, D_FF, D_MODEL), D_FF ** -0.5),
        "mix_norm_g": gain(ks[6], (DEPTH, D_MODEL)),
        "w_in": nrm(ks[7], (DEPTH, D_MODEL, IN_WIDTH), D_MODEL ** -0.5),
        "lam_q1": nrm(ks[8], (DEPTH, DIFF_HEAD_DIM), 0.1),
        "lam_k1": nrm(ks[9], (DEPTH, DIFF_HEAD_DIM), 0.1),
        "lam_q2": nrm(ks[10], (DEPTH, DIFF_HEAD_DIM), 0.1),
        "lam_k2": nrm(ks[11], (DEPTH, DIFF_HEAD_DIM), 0.1),
        "subln_g": gain(ks[12], (DEPTH, DIFF_V_DIM)),
        "w_pool": nrm(ks[13], (DEPTH, N_POOL_GROUPS, POOL_GROUP_WIDTH, POOL_GROUP_WIDTH), POOL_GROUP_WIDTH ** -0.5),
        "pool_scale": gain(ks[14], (DEPTH, POOL_WIDTH)),
        "w_out": nrm(ks[15], (DEPTH, D_MODEL, D_MODEL), D_MODEL ** -0.5),
        "ffn2_norm_g": gain(ks[16], (DEPTH, D_MODEL)),
        "ffn2_w_gate": nrm(ks[17], (DEPTH, D_MODEL, D_FF), D_MODEL ** -0.5),
        "ffn2_w_up": nrm(ks[18], (DEPTH, D_MODEL, D_FF), D_MODEL ** -0.5),
        "ffn2_w_down": nrm(ks[19], (DEPTH, D_FF, D_MODEL), D_FF ** -0.5),
        "final_norm_g": gain(ks[20], (D_MODEL,)),
    }


def reference(x, meta_tokens, ffn1_norm_g, ffn1_w_gate, ffn1_w_up, ffn1_w_down, mix_norm_g, w_in,
              lam_q1, lam_k1, lam_q2, lam_k2, subln_g, w_pool, pool_scale, w_out,
              ffn2_norm_g, ffn2_w_gate, ffn2_w_up, ffn2_w_down, final_norm_g):
    b, seq, d = x.shape
    total = N_META + seq
    pad = (-total) % Q_BLOCK
    meta = jnp.broadcast_to(meta_tokens.astype(x.dtype)[None], (b, N_META, d))
    h = jnp.concatenate([meta, x, jnp.zeros((b, pad, d), x.dtype)], axis=1)
    cos, sin = rope_tables(h.shape[1])
    for layer in range(DEPTH):
        lam_init = 0.8 - 0.6 * math.exp(-0.3 * layer)
        h = h + 0.5 * swiglu(rms_norm(h, ffn1_norm_g[layer], NORM_EPS),
                             ffn1_w_gate[layer], ffn1_w_up[layer], ffn1_w_down[layer])
        h = h + hybrid_mixer(rms_norm(h, mix_norm_g[layer], NORM_EPS), w_in[layer],
                             lam_q1[layer], lam_k1[layer], lam_q2[layer], lam_k2[layer], subln_g[layer],
                             w_pool[layer], pool_scale[layer], w_out[layer], cos, sin, lam_init)
        h = h + 0.5 * swiglu(rms_norm(h, ffn2_norm_g[layer], NORM_EPS),
                             ffn2_w_gate[layer], ffn2_w_up[layer], ffn2_w_down[layer])
    h = rms_norm(h, final_norm_g, NORM_EPS)
    return h[:, N_META:N_META + seq]
```

```python
import math
from contextlib import ExitStack

import numpy as np
import concourse.bass as bass
import concourse.mybir as mybir
from concourse.bass_utils import run_bass_kernel_spmd

F32 = mybir.dt.float32
BF16 = mybir.dt.bfloat16
AF = mybir.ActivationFunctionType
ALU = mybir.AluOpType
AX = mybir.AxisListType

D = 2048
DFF = 5632
KC = D // 128
FC = DFF // 128
NMETA = 16
NHEAD = 8
NORM_EPS = 1e-6
SUBLN_EPS = 1e-5
WINDOWS = (2, 4, 8, 16)
WT = 2048


def split_sub(G):
    ns = (G + 511) // 512
    base = (G + ns - 1) // ns
    out = []
    o = 0
    while o < G:
        n = min(base, G - o)
        out.append((o, n))
        o += n
    return out


class Cfg:
    def __init__(self, depth=4, nseq=2, seqlen=2048, ffn_g=688):
        self.depth = depth
        self.nseq = nseq
        self.L = NMETA + seqlen
        self.NT = nseq * self.L
        self.ffn_groups = []
        o = 0
        while o < self.NT:
            g = min(ffn_g, self.NT - o)
            self.ffn_groups.append((o, g))
            o += g
        self.GF = max(g for _, g in self.ffn_groups)
        self.mix_groups = []
        o = 0
        while o < self.L:
            g = min(512, self.L - o)
            if self.L - (o + g) < 128 and self.L - (o + g) > 0:
                g = self.L - o
            self.mix_groups.append((o, g))
            o += g
        self.GM = max(g for _, g in self.mix_groups)
        self.nblk = (self.L + 127) // 128


class VSem:
    LIMIT = 30000

    def __init__(self, S, step):
        self.S = S
        self.step = step
        self.hw = None
        self.cnt = 0

    def bump(self):
        if self.hw is None or self.cnt + self.step > self.LIMIT:
            self.hw = self.S.new_hw()
            self.cnt = 0
        self.cnt += self.step
        return (self.hw, self.cnt)


class _Rec:
    def __getattr__(self, name):
        def f(*a, **k):
            self.call = (name, a, k)
        return f


class Sched:
    ENGS = ("sp", "act", "dve", "pool", "pe")

    def __init__(self, nc, es):
        self.nc = nc
        self.es = es
        self.q = {e: [] for e in self.ENGS}
        self.nsem = 0
        self.vs = {e: VSem(self, 1) for e in ("pe", "act", "dve")}

    def new_hw(self):
        self.nsem += 1
        return self.es.enter_context(self.nc.semaphore(f"sem{self.nsem}"))

    @staticmethod
    def _norm(waits):
        best = {}
        for w in waits:
            if w is None:
                continue
            hw, v = w
            k = id(hw)
            if k not in best or best[k][1] < v:
                best[k] = (hw, v)
        return list(best.values())

    def op(self, eng, fn, waits=(), tok=True):
        r = _Rec()
        fn(r)
        t = self.vs[eng].bump() if tok else None
        self.q[eng].append((r.call, self._norm(waits), t, 1))
        return t

    def dma(self, eng, fn, vsem, waits=()):
        r = _Rec()
        fn(r)
        t = vsem.bump()
        self.q[eng].append((r.call, self._norm(waits), t, 16))
        return t

    def emit(self, block):
        def runner(lst):
            def f(e):
                seen = {}
                for (meth, a, kw), waits, t, step in lst:
                    for hw, v in waits:
                        k = id(hw)
                        if seen.get(k, 0) >= v:
                            continue
                        e.wait_ge(hw, v)
                        seen[k] = v
                    ins = getattr(e, meth)(*a, **kw)
                    if t is not None:
                        ins.then_inc(t[0], step)
            return f

        block.sync(runner(self.q["sp"]))
        block.scalar(runner(self.q["act"]))
        block.vector(runner(self.q["dve"]))
        block.gpsimd(runner(self.q["pool"]))
        block.tensor(runner(self.q["pe"]))


class Ring:
    def __init__(self, S, tiles, dma=False):
        self.tiles = tiles
        self.free = [[] for _ in tiles]
        self.i = 0
        self.held = set()
        self.vs = [VSem(S, 16) for _ in tiles] if dma else None

    def get(self):
        idx = self.i % len(self.tiles)
        self.i += 1
        assert idx not in self.held, "ring buffer re-acquired before release"
        self.held.add(idx)
        w = self.free[idx]
        self.free[idx] = []
        return idx, self.tiles[idx], w

    def release(self, idx, tok):
        assert tok is not None
        self.held.discard(idx)
        self.free[idx].append(tok)


def build_program(cfg):
    nc = bass.Bass("TRN2", target_bir_lowering=False)
    NT, L, DEPTH = cfg.NT, cfg.L, cfg.depth
    GF, GM = cfg.GF, cfg.GM
    GX = max(GF, GM)
    NBLK = cfg.nblk

    def din(name, shape):
        return nc.dram_tensor(name, shape, F32, kind="ExternalInput").ap()

    xT = din("xT", [KC, 128, NT])
    outT = nc.dram_tensor("outT", [KC, 128, NT], F32, kind="ExternalOutput").ap()
    hT = nc.dram_tensor("hT", [KC, 128, NT], F32, kind="Internal").ap()
    w_gu = din("w_gu", [DEPTH * 2 * 2 * FC, 128, WT])
    w_dn = din("w_dn", [DEPTH * 2 * KC * 4, 128, 11 * 128])
    w_in = din("w_in", [DEPTH * 32, 128, WT])
    w_pl = din("w_pl", [DEPTH, 128, WT])
    w_ot = din("w_ot", [DEPTH * KC, 128, WT])
    cvec = din("cvec", [128, (3 * DEPTH + 1) * KC + DEPTH * 8])
    gsub = din("gsub", [128, DEPTH * 128])
    lamv = din("lamv", [128, DEPTH * 256])
    cmat = din("cmat", [128, 128 + 128 + 64 + 128])
    ropec = din("ropec", [128, L])
    ropes = din("ropes", [128, L])
    NCV = (3 * DEPTH + 1) * KC + DEPTH * 8

    es = ExitStack()
    S = Sched(nc, es)

    def sb(name, shape, dt):
        return es.enter_context(nc.sbuf_tensor(name, shape, dt))

    NBW = 8
    wbufs = [sb(f"wb{i}", [128, WT], BF16) for i in range(NBW)]
    cv = sb("cv", [128, NCV], F32)
    gsb = sb("gsb", [128, DEPTH * 128], F32)
    invc_t = sb("invc_t", [128, 64], F32)
    ones_bf = sb("ones_bf", [128, 128], BF16)
    ident = sb("ident", [128, 128], BF16)
    tri = sb("tri", [128, 128], BF16)
    rmat = sb("rmat", [128, 128], BF16)
    neglam = sb("neglam", [128, DEPTH], F32)
    lamtmp = sb("lamtmp", [128, 64], F32)
    lams = sb("lams", [128, 4], F32)

    xn = sb("xn", [128, KC, GX], BF16)
    rstd = sb("rstd", [128, GX], F32)
    hN = [sb(f"hN{i}", [128, GX], F32) for i in range(2)]
    hR = [sb(f"hR{i}", [128, GX], F32) for i in range(2)]
    ot = [sb(f"ot{i}", [128, GX], F32) for i in range(2)]
    sqf = sb("sqf", [128, GX], F32)
    acc = sb("acc", [128, GX], F32)
    hi_t = sb("hi_t", [128, GX], BF16)
    lo_t = sb("lo_t", [128, GX], BF16)
    FFN_BYTES = FC * GF * 2 + 2 * GF * 4
    vaug_n = NBLK * NHEAD * 129
    MIX_LAYOUT = [
        ("KT", NHEAD * L, BF16), ("VA", vaug_n, BF16), ("catT", KC * GM, BF16), ("QT", NHEAD * GM, BF16),
        ("cosg", GM, F32), ("sing", GM, F32), ("t1", GM, F32), ("t2", GM, F32), ("t3", GM, F32),
        ("qb0", GM, BF16), ("qb1", GM, BF16),
        ("vT0", GM, BF16), ("vT1", GM, BF16), ("ub", 16 + GM, F32), ("ua", 16 + GM, F32), ("uc", 16 + GM, F32),
        ("df", 2 * GM, BF16), ("uh", 8 * 16, F32),
        ("E0", 512, BF16), ("E1", 512, BF16), ("E2", 512, BF16), ("E3", 512, BF16),
        ("o0", 128, F32), ("o1", 128, F32), ("so", 128, F32), ("on0", 128, BF16), ("on1", 128, BF16),
        ("rc0", 4, F32), ("rc1", 4, F32),
    ]
    sz = {BF16: 2, F32: 4}
    MIX_BYTES = sum(((n * sz[dt] + 31) // 32) * 32 for _, n, dt in MIX_LAYOUT)
    UB = max(FFN_BYTES, MIX_BYTES) + 64
    uni = sb("uni", [128, UB // 4], F32)

    def carve(layout):
        out = {}
        off = 0
        for name, n, dt in layout:
            nb = ((n * sz[dt] + 31) // 32) * 32
            a = uni[:, off // 4:(off + nb) // 4]
            if dt == BF16:
                a = a.bitcast(BF16)
            out[name] = a[:, 0:n]
            off += nb
        return out

    FM = carve([("aT", FC * GF, BF16), ("sg0", GF, F32), ("sg1", GF, F32)])
    SM = carve([("lmv", DEPTH * 256, F32), ("cmf", 448, F32)])
    lmv, cmf = SM["lmv"], SM["cmf"]
    MM = carve(MIX_LAYOUT)
    aT = FM["aT"].rearrange("p (c g) -> p c g", c=FC)
    sgs = [FM["sg0"], FM["sg1"]]
    KT = MM["KT"].rearrange("p (h t) -> p h t", h=NHEAD)
    VA = MM["VA"].rearrange("p (b h e) -> p b h e", b=NBLK, h=NHEAD)
    catT = MM["catT"].rearrange("p (c g) -> p c g", c=KC)
    QT = MM["QT"].rearrange("p (h g) -> p h g", h=NHEAD)
    df = MM["df"].rearrange("p (i g) -> p i g", i=2)
    uh = MM["uh"].rearrange("p (c x) -> p c x", c=8)

    psb = [es.enter_context(nc.psum_tensor(f"ps{i}", [128, 512], F32)) for i in range(8)]

    wring = Ring(S, wbufs, dma=True)
    wstate = {"n": 0}

    def wreq(src, n):
        idx, buf, w = wring.get()
        t = S.dma("pool", lambda e, buf=buf, src=src, n=n: e.dma_start(out=buf[:, 0:n], in_=src),
                  wring.vs[idx], w)
        return idx, buf, t

    hNr = Ring(S, hN, dma=True)
    hRr = Ring(S, hR, dma=True)
    otr = Ring(S, ot, dma=True)
    misc_vs = VSem(S, 16)

    def sp_load(ring, src, n, extra=()):
        idx, buf, w = ring.get()
        t = S.dma("sp", lambda e, buf=buf, src=src, n=n: e.dma_start(out=buf[:, 0:n], in_=src),
                  ring.vs[idx], list(w) + list(extra))
        return idx, buf, t

    class PBanks:
        def __init__(self, ids):
            self.ids = ids
            self.free = {i: [] for i in ids}
            self.i = 0
            self.held = set()

        def get(self):
            b = self.ids[self.i % len(self.ids)]
            self.i += 1
            assert b not in self.held, f"PSUM bank {b} re-acquired before release"
            self.held.add(b)
            w = self.free[b]
            self.free[b] = []
            return b, w

        def release(self, b, tok):
            assert tok is not None
            self.held.discard(b)
            self.free[b].append(tok)

    def mm_group(out_ap, pairs, waits):
        n = len(pairs)
        t = None
        for i, (l, r) in enumerate(pairs):
            t = S.op("pe", lambda e, l=l, r=r, i=i: e.matmul(out_ap, l, r, start=(i == 0), stop=(i == n - 1)),
                     waits if i == 0 else (), tok=(i == n - 1))
        return t

    state = {"xn_rd": None, "last_store": []}

    t_cv = S.dma("sp", lambda e: e.dma_start(out=cv[:], in_=cvec), misc_vs)
    t_gs = S.dma("sp", lambda e: e.dma_start(out=gsb[:], in_=gsub), VSem(S, 16))
    t_lm = S.dma("sp", lambda e: e.dma_start(out=lmv[:], in_=lamv), VSem(S, 16))
    t_cm = S.dma("sp", lambda e: e.dma_start(out=cmf[:], in_=cmat), VSem(S, 16))
    t_one = S.op("dve", lambda e: e.memset(ones_bf[:], 1.0))
    t_id = S.op("dve", lambda e: e.tensor_copy(out=ident[:], in_=cmf[:, 0:128]), [t_cm])
    t_tri = S.op("dve", lambda e: e.tensor_copy(out=tri[:], in_=cmf[:, 128:256]), [t_cm])
    t_rm = S.op("dve", lambda e: e.tensor_copy(out=rmat[:], in_=cmf[:, 320:448]), [t_cm])
    t_inv = S.op("dve", lambda e: e.tensor_copy(out=invc_t[:], in_=cmf[:, 256:320]), [t_cm])
    invc = invc_t[:, :]
    t_setup = [t_cv, t_one, t_id, t_tri, t_inv, t_rm]
    tc = None
    for l in range(DEPTH):
        lam_init = 0.8 - 0.6 * math.exp(-0.3 * l)
        b = l * 256
        ta = S.op("dve", lambda e, b=b: e.tensor_tensor(out=lamtmp[:], in0=lmv[:, b:b + 64], in1=lmv[:, b + 64:b + 128], op=ALU.mult), [t_lm, tc])
        ta = S.op("dve", lambda e: e.reduce_sum(out=lams[:, 0:1], in_=lamtmp[:], axis=AX.X), [ta])
        tb = S.op("dve", lambda e, b=b: e.tensor_tensor(out=lamtmp[:], in0=lmv[:, b + 128:b + 192], in1=lmv[:, b + 192:b + 256], op=ALU.mult), [ta])
        tb = S.op("dve", lambda e: e.reduce_sum(out=lams[:, 1:2], in_=lamtmp[:], axis=AX.X), [tb])
        te = S.op("act", lambda e: e.activation(out=lams[:, 2:4], in_=lams[:, 0:2], func=AF.Exp), [tb])
        tc = S.op("dve", lambda e: e.tensor_tensor(out=lams[:, 0:1], in0=lams[:, 3:4], in1=lams[:, 2:3], op=ALU.subtract), [te])
        tc = S.op("dve", lambda e, l=l, li=lam_init: e.tensor_scalar(out=neglam[:, l:l + 1], in0=lams[:, 0:1], scalar1=-li, scalar2=None, op0=ALU.add), [tc])
        tg = S.op("dve", lambda e, l=l, li=lam_init: e.tensor_scalar(out=gsb[:, l * 128:(l + 1) * 128], in0=gsb[:, l * 128:(l + 1) * 128], scalar1=1.0 - li, scalar2=None, op0=ALU.mult), [t_gs])
        t_setup += [tc, tg]

    def barrier():
        toks = list(state["last_store"]) + list(t_setup)
        state["last_store"] = []
        for e in ("sp", "act", "dve", "pe"):
            S.op(e, (lambda en: (lambda eng: eng.nop()))(e), toks, tok=False)

    def norm_steps(src, c0, G, gcol, banks, final_dst=None, res=None):
        subs = split_sub(G)
        tacc = None
        for k in range(KC):
            i, buf, tl = sp_load(hNr, src[k, :, c0:c0 + G], G)
            if k == 0:
                tacc = S.op("act", lambda e, buf=buf: e.activation(out=acc[:, 0:G], in_=buf[:, 0:G], func=AF.Square), [tl, state.get("acc_rd")])
                hNr.release(i, tacc)
            else:
                ts = S.op("act", lambda e, buf=buf: e.activation(out=sqf[:, 0:G], in_=buf[:, 0:G], func=AF.Square), [tl, state.get("sq_rd")])
                hNr.release(i, ts)
                tacc = S.op("dve", lambda e: e.tensor_tensor(out=acc[:, 0:G], in0=acc[:, 0:G], in1=sqf[:, 0:G], op=ALU.add), [ts, tacc])
                state["sq_rd"] = tacc
            yield
        th = S.op("dve", lambda e: e.tensor_copy(out=hi_t[:, 0:G], in_=acc[:, 0:G]), [tacc, state.get("hilo_rd")])
        t2 = S.op("dve", lambda e: e.tensor_tensor(out=acc[:, 0:G], in0=acc[:, 0:G], in1=hi_t[:, 0:G], op=ALU.subtract), [th])
        tlo = S.op("dve", lambda e: e.tensor_copy(out=lo_t[:, 0:G], in_=acc[:, 0:G]), [t2])
        state["acc_rd"] = tlo
        tr = None
        for s, (off, n) in enumerate(subs):
            bk, w = banks.get()
            tp = mm_group(psb[bk][:, 0:n], [(ones_bf[:, :], hi_t[:, off:off + n]), (ones_bf[:, :], lo_t[:, off:off + n])], [th, tlo] + w)
            state["hilo_rd"] = tp
            t1 = S.op("act", lambda e, bk=bk, off=off, n=n: e.activation(out=rstd[:, off:off + n], in_=psb[bk][:, 0:n], func=AF.Ln, scale=1.0 / D, bias=eps_ap), [tp, state.get("rstd_rd")])
            banks.release(bk, t1)
            tr = S.op("act", lambda e, off=off, n=n: e.activation(out=rstd[:, off:off + n], in_=rstd[:, off:off + n], func=AF.Exp, scale=-0.5), [t1])
        yield
        outs = []
        for k in range(KC):
            i, buf, tl = sp_load(hNr, src[k, :, c0:c0 + G], G)
            if final_dst is None:
                tx = S.op("dve", lambda e, buf=buf, k=k: e.scalar_tensor_tensor(
                    out=xn[:, k, 0:G], in0=buf[:, 0:G], scalar=cv[:, gcol + k:gcol + k + 1], in1=rstd[:, 0:G],
                    op0=ALU.mult, op1=ALU.mult), [tl, tr, state["xn_rd"]])
                hNr.release(i, tx)
                outs.append(tx)
                state["rstd_rd"] = tx
            else:
                oi, obuf, ow = otr.get()
                tx = S.op("dve", lambda e, buf=buf, obuf=obuf, k=k: e.scalar_tensor_tensor(
                    out=obuf[:, 0:G], in0=buf[:, 0:G], scalar=cv[:, gcol + k:gcol + k + 1], in1=rstd[:, 0:G],
                    op0=ALU.mult, op1=ALU.mult), [tl, tr] + ow)
                hNr.release(i, tx)
                state["rstd_rd"] = tx
                tst = S.dma("sp", lambda e, obuf=obuf, k=k: e.dma_start(out=final_dst[k, :, c0:c0 + G], in_=obuf[:, 0:G]), otr.vs[oi], [tx])
                otr.release(oi, tst)
                state["last_store"].append(tst)
            yield
        if res is not None:
            res["xtok"] = outs

    def norm_pass(src, c0, G, gcol, banks, final_dst=None):
        res = {}
        for _ in norm_steps(src, c0, G, gcol, banks, final_dst, res):
            pass
        return res["xtok"]

    def advance(gen, n):
        if gen is None:
            return
        for _ in range(n):
            try:
                next(gen)
            except StopIteration:
                return

    eps_t = sb("eps_t", [128, 2], F32)
    S.op("dve", lambda e: e.memset(eps_t[:, 0:1], NORM_EPS), tok=False)
    t_eps = S.op("dve", lambda e: e.memset(eps_t[:, 1:2], SUBLN_EPS))
    t_setup.append(t_eps)
    eps_ap = eps_t[:, 0:1]
    eps2_ap = eps_t[:, 1:2]

    def ffn_phase(l, f, src, dst):
        banks = PBanks(list(range(8)))
        gcol = (l * 3 + (0 if f == 0 else 2)) * KC
        gu_base = ((l * 2 + f) * 2) * FC
        dn_base = (l * 2 + f) * KC * 4
        sgr = Ring(S, sgs)
        groups = cfg.ffn_groups
        nres = {}
        advance(norm_steps(src, groups[0][0], groups[0][1], gcol, banks, None, nres), 1000)
        for gi_, (c0, G) in enumerate(groups):
            subs = split_sub(G)
            xtok = nres["xtok"]
            a_tok = []
            last_gu = None
            for c in range(FC):
                gi, gb, tg = wreq(w_gu[gu_base + c], WT)
                ui, ubf, tu = wreq(w_gu[gu_base + FC + c], WT)
                gbk = []
                ubk = []
                for wb, tw, lst in ((gb, tg, gbk), (ubf, tu, ubk)):
                    for s, (off, n) in enumerate(subs):
                        bk, w = banks.get()
                        t = mm_group(psb[bk][:, 0:n],
                                     [(wb[:, k * 128:(k + 1) * 128], xn[:, k, off:off + n]) for k in range(KC)],
                                     [tw] + w + (xtok if c == 0 else []))
                        lst.append((bk, t))
                last_gu = ubk[-1][1]
                wring.release(gi, gbk[-1][1])
                wring.release(ui, last_gu)
                si, sgb, sw = sgr.get()
                for s, (off, n) in enumerate(subs):
                    bk, t = gbk[s]
                    t1 = S.op("act", lambda e, sgb=sgb, bk=bk, off=off, n=n: e.activation(out=sgb[:, off:off + n], in_=psb[bk][:, 0:n], func=AF.Silu), [t] + sw)
                    banks.release(bk, t1)
                    bk2, t2 = ubk[s]
                    t3 = S.op("dve", lambda e, sgb=sgb, bk2=bk2, off=off, n=n, c=c: e.tensor_tensor(out=aT[:, c, off:off + n], in0=psb[bk2][:, 0:n], in1=sgb[:, off:off + n], op=ALU.mult), [t1, t2])
                    banks.release(bk2, t3)
                sgr.release(si, t3)
                a_tok.append(t3)
            state["xn_rd"] = last_gu
            ngen = None
            if gi_ + 1 < len(groups):
                nres = {}
                ngen = norm_steps(src, groups[gi_ + 1][0], groups[gi_ + 1][1], gcol, banks, None, nres)
            pre = {}
            for j in range(min(2, KC)):
                pre[j] = sp_load(hRr, src[j, :, c0:c0 + G], G)
            for j in range(KC):
                tiles = []
                for qd in range(4):
                    tiles.append(wreq(w_dn[dn_base + j * 4 + qd], 11 * 128))
                bks = []
                for s, (off, n) in enumerate(subs):
                    bk, w = banks.get()
                    pairs = []
                    for c in range(FC):
                        wb = tiles[c // 11][1]
                        cc = c % 11
                        pairs.append((wb[:, cc * 128:(cc + 1) * 128], aT[:, c, off:off + n]))
                    t = mm_group(psb[bk][:, 0:n], pairs, [tt[2] for tt in tiles] + w + (a_tok if j == 0 else []))
                    bks.append((bk, t))
                for tt in tiles:
                    wring.release(tt[0], bks[-1][1])
                ri, rbuf, rt = pre.pop(j)
                oi, obuf, ow = otr.get()
                for s, (off, n) in enumerate(subs):
                    bk, t = bks[s]
                    te = S.op("dve", lambda e, obuf=obuf, rbuf=rbuf, bk=bk, off=off, n=n: e.scalar_tensor_tensor(
                        out=obuf[:, off:off + n], in0=psb[bk][:, 0:n], scalar=0.5, in1=rbuf[:, off:off + n],
                        op0=ALU.mult, op1=ALU.add), [t, rt] + ow)
                    banks.release(bk, te)
                hRr.release(ri, te)
                if j + 2 < KC:
                    pre[j + 2] = sp_load(hRr, src[j + 2, :, c0:c0 + G], G)
                tst = S.dma("sp", lambda e, obuf=obuf, j=j: e.dma_start(out=dst[j, :, c0:c0 + G], in_=obuf[:, 0:G]), otr.vs[oi], [te])
                otr.release(oi, tst)
                state["last_store"].append(tst)
                advance(ngen, 3)
            advance(ngen, 1000)

    Er = Ring(S, [MM["E0"], MM["E1"], MM["E2"], MM["E3"]])
    o_r = Ring(S, [MM["o0"], MM["o1"]])
    on_r = Ring(S, [MM["on0"], MM["on1"]])
    rc_r = Ring(S, [MM["rc0"], MM["rc1"]])
    vT_r = Ring(S, [MM["vT0"], MM["vT1"]])

    def mixer_phase(l):
        pbanks = PBanks([0, 1, 2])
        sbanks = PBanks([3, 4, 5])
        obanks = PBanks([6, 7])
        qb_r = Ring(S, [MM["qb0"], MM["qb1"]])
        gcol = (l * 3 + 1) * KC
        pscol = (3 * DEPTH + 1) * KC + l * 8
        cos_vs, sin_vs = VSem(S, 16), VSem(S, 16)
        wl_vs = VSem(S, 16)
        rope_last = [None]
        t_va = None
        allg = [(s_, t_, g_) for s_ in range(cfg.nseq) for (t_, g_) in cfg.mix_groups]
        nres = {}
        advance(norm_steps(hT, allg[0][0] * L + allg[0][1], allg[0][2], gcol, pbanks, None, nres), 1000)
        pend_t = []

        def flush_t():
            while pend_t:
                pend_t.pop(0)()

        for sq_i in range(cfg.nseq):
            t_va = S.op("dve", lambda e: e.memset(MM["VA"], 1.0))
            t_uh = S.op("dve", lambda e: e.memset(MM["uh"], 0.0))
            for gidx, (t0, G) in enumerate(cfg.mix_groups):
                c0 = sq_i * L + t0
                gflat = sq_i * len(cfg.mix_groups) + gidx
                subs = split_sub(G)
                blocks = []
                o = 0
                while o < G:
                    nq = min(128, G - o)
                    blocks.append(((t0 + o) // 128, o, nq))
                    o += nq
                xtok = nres["xtok"]
                tcs = S.dma("sp", lambda e, t0=t0, G=G: e.dma_start(out=MM["cosg"][:, 0:G], in_=ropec[:, t0:t0 + G]), cos_vs, [rope_last[0]])
                tsn = S.dma("sp", lambda e, t0=t0, G=G: e.dma_start(out=MM["sing"][:, 0:G], in_=ropes[:, t0:t0 + G]), sin_vs, [rope_last[0]])
                first_proj = [True]

                def proj(widx):
                    wi, wb, tw = wreq(w_in[l * 32 + widx], WT)
                    res = []
                    for s, (off, n) in enumerate(subs):
                        bk, w = pbanks.get()
                        t = mm_group(psb[bk][:, 0:n],
                                     [(wb[:, k * 128:(k + 1) * 128], xn[:, k, off:off + n]) for k in range(KC)],
                                     [tw] + w + (xtok if first_proj[0] else []))
                        first_proj[0] = False
                        res.append((bk, t))
                    wring.release(wi, res[-1][1])
                    state["xn_rd"] = res[-1][1]
                    return res

                def rope1(pa, t1buf):
                    qi, qbuf, qw = qb_r.get()
                    ta_ = None
                    for s, (off, n) in enumerate(subs):
                        b1, ta = pa[s]
                        tc_ = S.op("act", lambda e, b1=b1, off=off, n=n: e.activation(out=qbuf[:, off:off + n], in_=psb[b1][:, 0:n], func=AF.Copy), [ta] + qw)
                        ta_ = S.op("dve", lambda e, b1=b1, off=off, n=n: e.tensor_tensor(out=t1buf[:, off:off + n], in0=psb[b1][:, 0:n], in1=MM["cosg"][:, off:off + n], op=ALU.mult), [tc_, tcs, rope_last[0]])
                        pbanks.release(b1, ta_)
                    return qi, qbuf, tc_, ta_

                def rope2(r1, t1buf, dst_fn):
                    qi, qbuf, tc_, ta_ = r1
                    tl = None
                    tr_ = None
                    for s, (off, n) in enumerate(subs):
                        bk, w = pbanks.get()
                        tr_ = S.op("pe", lambda e, bk=bk, off=off, n=n: e.matmul(psb[bk][:, 0:n], rmat[:, :], qbuf[:, off:off + n], start=True, stop=True), [tc_] + w)
                        x2 = S.op("dve", lambda e, bk=bk, off=off, n=n: e.tensor_tensor(out=MM["t3"][:, off:off + n], in0=psb[bk][:, 0:n], in1=MM["sing"][:, off:off + n], op=ALU.mult), [tr_, tsn, rope_last[0]])
                        pbanks.release(bk, x2)
                        tl = S.op("dve", lambda e, off=off, n=n: e.tensor_tensor(out=dst_fn(off, n), in0=t1buf[:, off:off + n], in1=MM["t3"][:, off:off + n], op=ALU.add), [x2, ta_])
                        rope_last[0] = tl
                    qb_r.release(qi, tr_)
                    return tl

                for h in range(NHEAD):
                    pq = proj(h)
                    r1q = rope1(pq, MM["t1"])
                    pk = proj(8 + h)
                    r1k = rope1(pk, MM["t2"])
                    tq = rope2(r1q, MM["t1"], lambda off, n, h=h: QT[:, h, off:off + n])
                    tk = rope2(r1k, MM["t2"], lambda off, n, h=h: KT[:, h, t0 + off:t0 + off + n])
                    flush_t()
                    pv = proj(16 + h)
                    vi, vbuf, vw = vT_r.get()
                    tv = None
                    for s, (off, n) in enumerate(subs):
                        bk, t = pv[s]
                        tv = S.op("act", lambda e, vbuf=vbuf, bk=bk, off=off, n=n: e.activation(out=vbuf[:, off:off + n], in_=psb[bk][:, 0:n], func=AF.Copy), [t] + vw)
                        pbanks.release(bk, tv)
                    tvl = []
                    tlast = None
                    tb_id, w = pbanks.get()
                    tbank = psb[tb_id][:, :].bitcast(BF16)
                    for sl, (gb, o, nq) in enumerate(blocks):
                        tlast = S.op("pe", lambda e, sl=sl, vbuf=vbuf, o=o, nq=nq: e.transpose(tbank[0:nq, sl * 128:(sl + 1) * 128], vbuf[:, o:o + nq], ident[:, :]), [tv] + w)
                    for sl, (gb, o, nq) in enumerate(blocks):
                        tcp = S.op("act", lambda e, sl=sl, gb=gb, nq=nq, h=h: e.activation(out=VA[0:nq, gb, h, 0:128], in_=tbank[0:nq, sl * 128:(sl + 1) * 128], func=AF.Copy), [tlast, t_va])
                        tvl.append(tcp)
                    pbanks.release(tb_id, tcp)
                    vT_r.release(vi, tlast)
                    chunks = []
                    for (gb, o, nq) in blocks:
                        for m in range(2):
                            kbs = list(range(gb + 1))
                            for ci in range(0, len(kbs), 4):
                                ch = kbs[ci:ci + 4]
                                if len(ch) > 1 and (ch[-1] + 1) * 128 > L:
                                    chunks.append((gb, o, nq, m, ch[:-1]))
                                    chunks.append((gb, o, nq, m, ch[-1:]))
                                else:
                                    chunks.append((gb, o, nq, m, ch))
                    cur_o = {}
                    sinfo = [None] * len(chunks)

                    def emit_S(ci):
                        gb, o, nq, m, ch = chunks[ci]
                        nk = min(128, L - ch[0] * 128)
                        bk, w = sbanks.get()
                        t = None
                        for i, kb in enumerate(ch):
                            t = S.op("pe", lambda e, bk=bk, i=i, kb=kb, nk=nk, nq=nq, m=m, o=o: e.matmul(
                                psb[bk][0:nk, i * nq:(i + 1) * nq],
                                KT[m * 64:(m + 1) * 64, h, kb * 128:kb * 128 + nk],
                                QT[m * 64:(m + 1) * 64, h, o:o + nq], start=True, stop=True),
                                ([tq, tk] + w) if i == 0 else (), tok=(i == len(ch) - 1))
                        ei, eb, ew = Er.get()
                        ncol = len(ch) * nq
                        te = S.op("act", lambda e, eb=eb, bk=bk, nk=nk, ncol=ncol: e.activation(out=eb[0:nk, 0:ncol], in_=psb[bk][0:nk, 0:ncol], func=AF.Exp, scale=0.125), [t] + ew)
                        sbanks.release(bk, te)
                        if ch[-1] == gb:
                            i = len(ch) - 1
                            te = S.op("dve", lambda e, eb=eb, i=i, nk=nk, nq=nq: e.tensor_tensor(out=eb[0:nk, i * nq:(i + 1) * nq], in0=eb[0:nk, i * nq:(i + 1) * nq], in1=tri[0:nk, 0:nq], op=ALU.mult), [te])
                        sinfo[ci] = (ei, eb, te, nk)

                    def emit_AV(ci):
                        gb, o, nq, m, ch = chunks[ci]
                        ei, eb, te, nk = sinfo[ci]
                        if (gb, m) == (gb, 0) and ch[0] == 0 and m == 0:
                            ob, w = obanks.get()
                            cur_o[gb] = (ob, w)
                        ob, w0 = cur_o[gb]
                        t = None
                        for i, kb in enumerate(ch):
                            w = [te] + tvl
                            if kb == 0:
                                w = w + w0
                            t = S.op("pe", lambda e, ob=ob, eb=eb, i=i, kb=kb, nk=nk, nq=nq, m=m, gb=gb: e.matmul(
                                psb[ob][0:nq, m * 256:m * 256 + 129],
                                eb[0:nk, i * nq:(i + 1) * nq],
                                VA[0:nk, kb, h, 0:129], start=(kb == 0), stop=(kb == gb)),
                                w if i == 0 or kb == 0 else (), tok=(i == len(ch) - 1))
                        Er.release(ei, t)
                        if m == 1 and ch[-1] == gb:
                            combine(gb, o, nq, ob, t)

                    def combine(gb, o, nq, ob, tav):
                        P = psb[ob]
                        ri, rc, rw = rc_r.get()
                        a1 = S.op("dve", lambda e: e.reciprocal(out=rc[0:nq, 0:1], in_=P[0:nq, 128:129]), [tav] + rw)
                        a2 = S.op("dve", lambda e: e.reciprocal(out=rc[0:nq, 1:2], in_=P[0:nq, 384:385]), [tav])
                        a3 = S.op("dve", lambda e: e.tensor_tensor(out=rc[0:nq, 1:2], in0=rc[0:nq, 1:2], in1=neglam[0:nq, l:l + 1], op=ALU.mult), [a2])
                        oi, ob_, ow = o_r.get()
                        a4 = S.op("dve", lambda e: e.tensor_scalar(out=ob_[0:nq, 0:128], in0=P[0:nq, 0:128], scalar1=rc[0:nq, 0:1], scalar2=None, op0=ALU.mult), [a1] + ow)
                        a5 = S.op("dve", lambda e: e.scalar_tensor_tensor(out=ob_[0:nq, 0:128], in0=P[0:nq, 256:384], scalar=rc[0:nq, 1:2], in1=ob_[0:nq, 0:128], op0=ALU.mult, op1=ALU.add), [a3, a4])
                        obanks.release(ob, a5)
                        a6 = S.op("dve", lambda e: e.tensor_tensor(out=MM["so"][0:nq, 0:128], in0=ob_[0:nq, 0:128], in1=ob_[0:nq, 0:128], op=ALU.mult), [a5, state.get("so_rd")])
                        a7 = S.op("dve", lambda e: e.reduce_sum(out=rc[0:nq, 2:3], in_=MM["so"][0:nq, 0:128], axis=AX.X), [a6])
                        state["so_rd"] = a7
                        a8 = S.op("act", lambda e: e.activation(out=rc[0:nq, 3:4], in_=rc[0:nq, 2:3], func=AF.Ln, scale=1.0 / 128, bias=eps2_ap[0:nq, :]), [a7])
                        a9 = S.op("act", lambda e: e.activation(out=rc[0:nq, 3:4], in_=rc[0:nq, 3:4], func=AF.Exp, scale=-0.5), [a8])
                        ni, nb, nw = on_r.get()
                        a10 = S.op("dve", lambda e: e.scalar_tensor_tensor(out=nb[0:nq, 0:128], in0=ob_[0:nq, 0:128], scalar=rc[0:nq, 3:4], in1=gsb[0:nq, l * 128:(l + 1) * 128], op0=ALU.mult, op1=ALU.mult), [a9] + nw)
                        o_r.release(oi, a10)
                        rc_r.release(ri, a10)
                        def later(h=h, o=o, nq=nq, nb=nb, ni=ni, a10=a10, cat_tok=cat_tok):
                            tb_id, w = pbanks.get()
                            tbank = psb[tb_id][:, :].bitcast(BF16)
                            a11 = S.op("pe", lambda e: e.transpose(tbank[0:128, 0:nq], nb[0:nq, 0:128], ident[0:nq, 0:nq]), [a10] + w)
                            on_r.release(ni, a11)
                            a12 = S.op("act", lambda e: e.activation(out=catT[:, h, o:o + nq], in_=tbank[0:128, 0:nq], func=AF.Copy), [a11, state["cat_rd"]])
                            pbanks.release(tb_id, a12)
                            cat_tok.append(a12)
                        flush_t()
                        pend_t.append(later)

                    if h == 0:
                        cat_tok = []
                        state["cat_tok"] = cat_tok
                    else:
                        cat_tok = state["cat_tok"]
                    emit_S(0)
                    if len(chunks) > 1:
                        emit_S(1)
                    for ci in range(len(chunks)):
                        if ci + 2 < len(chunks):
                            emit_S(ci + 2)
                        emit_AV(ci)

                flush_t()
                cat_tok = state["cat_tok"]
                ub, ua, uc = MM["ub"], MM["ua"], MM["uc"]
                pool_last = state.get("pool_last")
                for g in range(4):
                    w = WINDOWS[g]
                    dtoks = []
                    for ic in range(2):
                        ucx = 2 * g + ic
                        pu = proj(24 + ucx)
                        th = S.op("dve", lambda e, ucx=ucx: e.tensor_copy(out=ub[:, 0:16], in_=uh[:, ucx, :]), [t_uh, pool_last])
                        tu = None
                        for s, (off, n) in enumerate(subs):
                            bk, t = pu[s]
                            tu = S.op("act", lambda e, bk=bk, off=off, n=n: e.activation(out=ub[:, 16 + off:16 + off + n], in_=psb[bk][:, 0:n], func=AF.Copy), [t, pool_last])
                            pbanks.release(bk, tu)
                        th2 = S.op("dve", lambda e, ucx=ucx, G=G: e.tensor_copy(out=uh[:, ucx, :], in_=ub[:, G:G + 16]), [tu, th])
                        srcb, lvl, tprev = ub, 1, th2
                        tmps = [ua, uc]
                        ti = 0
                        while lvl < w:
                            dstb = tmps[ti % 2]
                            ti += 1
                            lo = 2 * lvl - 1
                            tprev = S.op("dve", lambda e, srcb=srcb, dstb=dstb, lo=lo, lvl=lvl, G=G: e.tensor_tensor(
                                out=dstb[:, lo:16 + G], in0=srcb[:, lo:16 + G], in1=srcb[:, lo - lvl:16 + G - lvl], op=ALU.add), [tprev])
                            srcb = dstb
                            lvl *= 2
                        td = S.op("dve", lambda e, srcb=srcb, ic=ic, w=w, G=G: e.scalar_tensor_tensor(
                            out=df[:, ic, 0:G], in0=srcb[:, 16:16 + G], scalar=1.0 / w, in1=ub[:, 16:16 + G],
                            op0=ALU.mult, op1=ALU.subtract), [tprev, state.get("df_rd")])
                        if t0 == 0:
                            other = tmps[ti % 2]
                            tf = S.op("dve", lambda e, srcb=srcb, other=other, g=g: e.tensor_tensor(out=other[:, 0:16], in0=srcb[:, 16:32], in1=invc[:, g * 16:(g + 1) * 16], op=ALU.mult), [td])
                            td = S.op("dve", lambda e, other=other, ic=ic: e.tensor_tensor(out=df[:, ic, 0:16], in0=other[:, 0:16], in1=ub[:, 16:32], op=ALU.subtract), [tf])
                        pool_last = td
                        dtoks.append(td)
                    if g == 0:
                        wi_p, wb_p, tw_p = wreq(w_pl[l], WT)
                    for oc in range(2):
                        res = []
                        for s, (off, n) in enumerate(subs):
                            bk, wv = pbanks.get()
                            pairs = []
                            for ic in range(2):
                                col = ((g * 2 + ic) * 2 + oc) * 128
                                pairs.append((wb_p[:, col:col + 128], df[:, ic, off:off + n]))
                            t = mm_group(psb[bk][:, 0:n], pairs, [tw_p] + wv + dtoks)
                            res.append((bk, t))
                        state["df_rd"] = res[-1][1]
                        cidx = 8 + 2 * g + oc
                        for s, (off, n) in enumerate(subs):
                            bk, t = res[s]
                            tcp = S.op("dve", lambda e, bk=bk, off=off, n=n, cidx=cidx: e.tensor_scalar(
                                out=catT[:, cidx, off:off + n], in0=psb[bk][:, 0:n], scalar1=cv[:, pscol + cidx - 8:pscol + cidx - 7],
                                scalar2=None, op0=ALU.mult), [t, state["cat_rd"]])
                            pbanks.release(bk, tcp)
                            cat_tok.append(tcp)
                    if g == 3:
                        wring.release(wi_p, state["df_rd"])
                state["pool_last"] = pool_last
                ngen = None
                if gflat + 1 < len(allg):
                    nres = {}
                    ns_, nt_, ng_ = allg[gflat + 1]
                    ngen = norm_steps(hT, ns_ * L + nt_, ng_, gcol, pbanks, None, nres)
                pre = {}
                for j in range(2):
                    pre[j] = sp_load(hRr, hT[j, :, c0:c0 + G], G)
                last_o = None
                for j in range(KC):
                    wi, wb, tw = wreq(w_ot[l * KC + j], WT)
                    bks = []
                    for s, (off, n) in enumerate(subs):
                        bk, wv = pbanks.get()
                        t = mm_group(psb[bk][:, 0:n],
                                     [(wb[:, c * 128:(c + 1) * 128], catT[:, c, off:off + n]) for c in range(KC)],
                                     [tw] + wv + (cat_tok if j == 0 else []))
                        bks.append((bk, t))
                    wring.release(wi, bks[-1][1])
                    last_o = bks[-1][1]
                    ri, rbuf, rt = pre.pop(j)
                    oi, obuf, ow = otr.get()
                    te = None
                    for s, (off, n) in enumerate(subs):
                        bk, t = bks[s]
                        te = S.op("dve", lambda e, obuf=obuf, rbuf=rbuf, bk=bk, off=off, n=n: e.tensor_tensor(
                            out=obuf[:, off:off + n], in0=psb[bk][:, 0:n], in1=rbuf[:, off:off + n], op=ALU.add), [t, rt] + ow)
                        pbanks.release(bk, te)
                    hRr.release(ri, te)
                    if j + 2 < KC:
                        pre[j + 2] = sp_load(hRr, hT[j + 2, :, c0:c0 + G], G)
                    tst = S.dma("sp", lambda e, obuf=obuf, j=j, c0=c0, G=G: e.dma_start(out=hT[j, :, c0:c0 + G], in_=obuf[:, 0:G]), otr.vs[oi], [te])
                    otr.release(oi, tst)
                    state["last_store"].append(tst)
                    advance(ngen, 3)
                advance(ngen, 1000)
                state["cat_rd"] = last_o

    state["cat_rd"] = None

    barrier()
    for l in range(DEPTH):
        ffn_phase(l, 0, xT if l == 0 else hT, hT)
        barrier()
        mixer_phase(l)
        barrier()
        ffn_phase(l, 1, hT, hT)
        barrier()
    fb = PBanks(list(range(8)))
    for (c0, G) in cfg.ffn_groups:
        norm_pass(hT, c0, G, 3 * DEPTH * KC, fb, final_dst=outT)
    S.op("sp", lambda e: e.nop(), state["last_store"], tok=False)

    import os
    if os.environ.get("K_DEBUG"):
        print("nsem", S.nsem, {k: len(v) for k, v in S.q.items()}, {k: (v.cnt) for k, v in S.vs.items()}, flush=True)
    with nc.Block() as block:
        S.emit(block)
    es.close()
    return nc


def _tile_w(W, nk, no):
    return np.ascontiguousarray(W.reshape(nk, 128, no, 128).transpose(2, 1, 0, 3)).reshape(no, 128, nk * 128)


def prep_weights(inp, depth):
    gu = np.empty((depth, 2, 2, FC, 128, WT), np.float32)
    dn = np.empty((depth, 2, KC, 4, 128, 11 * 128), np.float32)
    win = np.empty((depth, 32, 128, WT), np.float32)
    wpl = np.empty((depth, 128, WT), np.float32)
    wot = np.empty((depth, KC, 128, WT), np.float32)
    perm = np.arange(1024).reshape(16, 64)
    perm = np.concatenate([perm[:, 32:], perm[:, :32]], axis=1).reshape(-1)
    for l in range(depth):
        for f, pre in enumerate(("ffn1", "ffn2")):
            gu[l, f, 0] = _tile_w(np.asarray(inp[pre + "_w_gate"][l]), KC, FC)
            gu[l, f, 1] = _tile_w(np.asarray(inp[pre + "_w_up"][l]), KC, FC)
            d = _tile_w(np.asarray(inp[pre + "_w_down"][l]), FC, KC)
            dn[l, f] = d.reshape(KC, 128, 4, 11 * 128).transpose(0, 2, 1, 3)
        W = np.asarray(inp["w_in"][l])
        q, k, v, u = W[:, :1024], W[:, 1024:2048], W[:, 2048:3072], W[:, 3072:]
        win[l] = _tile_w(W, KC, 32)
        wp = np.asarray(inp["w_pool"][l])
        wpl[l] = wp.reshape(4, 2, 128, 2, 128).transpose(2, 0, 1, 3, 4).reshape(128, WT)
        wot[l] = _tile_w(np.asarray(inp["w_out"][l]), KC, KC)
    return {
        "w_gu": gu.reshape(-1, 128, WT), "w_dn": dn.reshape(-1, 128, 11 * 128),
        "w_in": win.reshape(-1, 128, WT), "w_pl": wpl, "w_ot": wot.reshape(-1, 128, WT),
    }


def prep_consts(inp, depth, L):
    def col(v):
        return np.asarray(v, np.float32).reshape(-1, 128).T
    cols = []
    for l in range(depth):
        cols += [col(inp["ffn1_norm_g"][l]), col(inp["mix_norm_g"][l]), col(inp["ffn2_norm_g"][l])]
    cols.append(col(inp["final_norm_g"]))
    for l in range(depth):
        cols.append(col(inp["pool_scale"][l]))
    cvec = np.ascontiguousarray(np.concatenate(cols, axis=1), np.float32)
    gsub = np.ascontiguousarray(np.broadcast_to(np.asarray(inp["subln_g"], np.float32)[:depth].reshape(1, -1), (128, depth * 128)))
    lam = np.stack([np.asarray(inp[k], np.float32)[:depth] for k in ("lam_q1", "lam_k1", "lam_q2", "lam_k2")], axis=1)
    lamv = np.ascontiguousarray(np.broadcast_to(lam.reshape(1, -1), (128, depth * 256)))
    cmat = np.zeros((128, 448), np.float32)
    cmat[:, 0:128] = np.eye(128, dtype=np.float32)
    kk = np.arange(128)
    cmat[:, 128:256] = (kk[None, :] >= kk[:, None]).astype(np.float32)
    for po in range(128):
        d = po % 64
        if d < 32:
            cmat[po + 32, 320 + po] = -1.0
        else:
            cmat[po - 32, 320 + po] = 1.0
    for g, w in enumerate(WINDOWS):
        t = np.arange(16)
        cmat[:, 256 + g * 16:256 + (g + 1) * 16] = (1.0 / np.minimum(t + 1, w)).astype(np.float32)[None, :]
    pos = np.arange(L, dtype=np.float32)
    inv_freq = (np.float32(1.0) / (np.float32(10000.0) ** (np.arange(0, 64, 2, dtype=np.float32) / np.float32(64)))).astype(np.float32)
    ang = (pos[:, None] * inv_freq[None, :]).astype(np.float32)
    c = np.cos(ang.astype(np.float64)).astype(np.float32).T
    s = np.sin(ang.astype(np.float64)).astype(np.float32).T
    ropec = np.ascontiguousarray(np.concatenate([c, c, c, c], axis=0))
    ropes = np.ascontiguousarray(np.concatenate([s, s, s, s], axis=0))
    return {"cvec": cvec, "gsub": gsub, "lamv": lamv, "cmat": cmat, "ropec": ropec, "ropes": ropes}


def run(inputs, cfg, ncores, trace=False):
    x = np.asarray(inputs["x"], np.float32)
    meta = np.asarray(inputs["meta_tokens"], np.float32)
    B, SEQ, _ = x.shape
    assert B == ncores * cfg.nseq and SEQ + NMETA == cfg.L
    shared = {}
    shared.update(prep_weights(inputs, cfg.depth))
    shared.update(prep_consts(inputs, cfg.depth, cfg.L))
    in_maps = []
    for c in range(ncores):
        cols = []
        for s in range(cfg.nseq):
            hb = np.concatenate([meta, x[c * cfg.nseq + s]], axis=0)
            cols.append(hb.T)
        xt = np.ascontiguousarray(np.concatenate(cols, axis=1)).reshape(KC, 128, cfg.NT)
        m = dict(shared)
        m["xT"] = xt
        in_maps.append(m)
    nc = build_program(cfg)
    res = run_bass_kernel_spmd(nc, in_maps, core_ids=list(range(ncores)), trace=trace)
    out = np.empty((B, SEQ, D), np.float32)
    for c in range(ncores):
        o = np.asarray(res.results[c]["outT"]).reshape(D, cfg.NT)
        for s in range(cfg.nseq):
            out[c * cfg.nseq + s] = o[:, s * cfg.L + NMETA:(s + 1) * cfg.L].T
    return out, res


def kernel(**inputs):
    cfg = Cfg(depth=4, nseq=2, seqlen=2048, ffn_g=688)
    out, _ = run(inputs, cfg, 8)
    return out
```

```python
import math
from contextlib import ExitStack

import numpy as np
import concourse.bass as bass
import concourse.mybir as mybir
from concourse.bass_utils import run_bass_kernel_spmd

F32 = mybir.dt.float32
BF16 = mybir.dt.bfloat16
AF = mybir.ActivationFunctionType
ALU = mybir.AluOpType
AX = mybir.AxisListType

D = 2048
DFF = 5632
KC = D // 128
FC = DFF // 128
NMETA = 16
NHEAD = 8
NORM_EPS = 1e-6
SUBLN_EPS = 1e-5
WINDOWS = (2, 4, 8, 16)
WT = 2048


def split_sub(G):
    ns = (G + 511) // 512
    base = (G + ns - 1) // ns
    out = []
    o = 0
    while o < G:
        n = min(base, G - o)
        out.append((o, n))
        o += n
    return out


class Cfg:
    def __init__(self, depth=4, nseq=2, seqlen=2048, ffn_g=688):
        self.depth = depth
        self.nseq = nseq
        self.L = NMETA + seqlen
        self.NT = nseq * self.L
        self.ffn_groups = []
        o = 0
        while o < self.NT:
            g = min(ffn_g, self.NT - o)
            self.ffn_groups.append((o, g))
            o += g
        self.GF = max(g for _, g in self.ffn_groups)
        self.mix_groups = []
        o = 0
        while o < self.L:
            g = min(512, self.L - o)
            if self.L - (o + g) < 128 and self.L - (o + g) > 0:
                g = self.L - o
            self.mix_groups.append((o, g))
            o += g
        self.GM = max(g for _, g in self.mix_groups)
        self.nblk = (self.L + 127) // 128


class VSem:
    LIMIT = 30000

    def __init__(self, S, step):
        self.S = S
        self.step = step
        self.hw = None
        self.cnt = 0

    def bump(self):
        if self.hw is None or self.cnt + self.step > self.LIMIT:
            self.hw = self.S.new_hw()
            self.cnt = 0
        self.cnt += self.step
        return (self.hw, self.cnt)


class _Rec:
    def __getattr__(self, name):
        def f(*a, **k):
            self.call = (name, a, k)
        return f


class Sched:
    ENGS = ("sp", "act", "dve", "pool", "pe")

    def __init__(self, nc, es):
        self.nc = nc
        self.es = es
        self.q = {e: [] for e in self.ENGS}
        self.nsem = 0
        self.vs = {e: VSem(self, 1) for e in ("pe", "act", "dve")}

    def new_hw(self):
        self.nsem += 1
        return self.es.enter_context(self.nc.semaphore(f"sem{self.nsem}"))

    @staticmethod
    def _norm(waits):
        best = {}
        for w in waits:
            if w is None:
                continue
            hw, v = w
            k = id(hw)
            if k not in best or best[k][1] < v:
                best[k] = (hw, v)
        return list(best.values())

    def op(self, eng, fn, waits=(), tok=True):
        r = _Rec()
        fn(r)
        t = self.vs[eng].bump() if tok else None
        self.q[eng].append((r.call, self._norm(waits), t, 1))
        return t

    def dma(self, eng, fn, vsem, waits=()):
        r = _Rec()
        fn(r)
        t = vsem.bump()
        self.q[eng].append((r.call, self._norm(waits), t, 16))
        return t

    def emit(self, block):
        def runner(lst):
            def f(e):
                seen = {}
                for (meth, a, kw), waits, t, step in lst:
                    for hw, v in waits:
                        k = id(hw)
                        if seen.get(k, 0) >= v:
                            continue
                        e.wait_ge(hw, v)
                        seen[k] = v
                    ins = getattr(e, meth)(*a, **kw)
                    if t is not None:
                        ins.then_inc(t[0], step)
            return f

        block.sync(runner(self.q["sp"]))
        block.scalar(runner(self.q["act"]))
        block.vector(runner(self.q["dve"]))
        block.gpsimd(runner(self.q["pool"]))
        block.tensor(runner(self.q["pe"]))


class Ring:
    def __init__(self, S, tiles, dma=False):
        self.tiles = tiles
        self.free = [[] for _ in tiles]
        self.i = 0
        self.held = set()
        self.vs = [VSem(S, 16) for _ in tiles] if dma else None

    def get(self):
        idx = self.i % len(self.tiles)
        self.i += 1
        assert idx not in self.held, "ring buffer re-acquired before release"
        self.held.add(idx)
        w = self.free[idx]
        self.free[idx] = []
        return idx, self.tiles[idx], w

    def release(self, idx, tok):
        assert tok is not None
        self.held.discard(idx)
        self.free[idx].append(tok)


def build_program(cfg):
    nc = bass.Bass("TRN2", target_bir_lowering=False)
    NT, L, DEPTH = cfg.NT, cfg.L, cfg.depth
    GF, GM = cfg.GF, cfg.GM
    GX = max(GF, GM)
    NBLK = cfg.nblk

    def din(name, shape):
        return nc.dram_tensor(name, shape, F32, kind="ExternalInput").ap()

    xT = din("xT", [KC, 128, NT])
    outT = nc.dram_tensor("outT", [KC, 128, NT], F32, kind="ExternalOutput").ap()
    hT = nc.dram_tensor("hT", [KC, 128, NT], F32, kind="Internal").ap()
    w_gu = din("w_gu", [DEPTH * 2 * 2 * FC, 128, WT])
    w_dn = din("w_dn", [DEPTH * 2 * KC * 4, 128, 11 * 128])
    w_in = din("w_in", [DEPTH * 32, 128, WT])
    w_pl = din("w_pl", [DEPTH, 128, WT])
    w_ot = din("w_ot", [DEPTH * KC, 128, WT])
    cvec = din("cvec", [128, (3 * DEPTH + 1) * KC + DEPTH * 8])
    gsub = din("gsub", [128, DEPTH * 128])
    lamv = din("lamv", [128, DEPTH * 256])
    cmat = din("cmat", [128, 128 + 128 + 64 + 128])
    ropec = din("ropec", [128, L])
    ropes = din("ropes", [128, L])
    NCV = (3 * DEPTH + 1) * KC + DEPTH * 8

    es = ExitStack()
    S = Sched(nc, es)

    def sb(name, shape, dt):
        return es.enter_context(nc.sbuf_tensor(name, shape, dt))

    NBW = 8
    wbufs = [sb(f"wb{i}", [128, WT], BF16) for i in range(NBW)]
    cv = sb("cv", [128, NCV], F32)
    gsb = sb("gsb", [128, DEPTH * 128], F32)
    invc_t = sb("invc_t", [128, 64], F32)
    ones_bf = sb("ones_bf", [128, 128], BF16)
    ident = sb("ident", [128, 128], BF16)
    tri = sb("tri", [128, 128], BF16)
    rmat = sb("rmat", [128, 128], BF16)
    neglam = sb("neglam", [128, DEPTH], F32)
    lamtmp = sb("lamtmp", [128, 64], F32)
    lams = sb("lams", [128, 4], F32)

    xn = sb("xn", [128, KC, GX], BF16)
    rstd = sb("rstd", [128, GX], F32)
    hN = [sb(f"hN{i}", [128, GX], F32) for i in range(2)]
    hR = [sb(f"hR{i}", [128, GX], F32) for i in range(2)]
    ot = [sb(f"ot{i}", [128, GX], F32) for i in range(2)]
    sqf = sb("sqf", [128, GX], F32)
    acc = sb("acc", [128, GX], F32)
    hi_t = sb("hi_t", [128, GX], BF16)
    lo_t = sb("lo_t", [128, GX], BF16)
    FFN_BYTES = FC * GF * 2 + 2 * GF * 4
    vaug_n = NBLK * NHEAD * 129
    MIX_LAYOUT = [
        ("KT", NHEAD * L, BF16), ("VA", vaug_n, BF16), ("catT", KC * GM, BF16), ("QT", NHEAD * GM, BF16),
        ("cosg", GM, F32), ("sing", GM, F32), ("t1", GM, F32), ("t2", GM, F32), ("t3", GM, F32),
        ("qb0", GM, BF16), ("qb1", GM, BF16),
        ("vT0", GM, BF16), ("vT1", GM, BF16), ("ub", 16 + GM, F32), ("ua", 16 + GM, F32), ("uc", 16 + GM, F32),
        ("df", 2 * GM, BF16), ("uh", 8 * 16, F32),
        ("E0", 512, BF16), ("E1", 512, BF16), ("E2", 512, BF16), ("E3", 512, BF16),
        ("o0", 128, F32), ("o1", 128, F32), ("so", 128, F32), ("on0", 128, BF16), ("on1", 128, BF16),
        ("rc0", 4, F32), ("rc1", 4, F32),
    ]
    sz = {BF16: 2, F32: 4}
    MIX_BYTES = sum(((n * sz[dt] + 31) // 32) * 32 for _, n, dt in MIX_LAYOUT)
    UB = max(FFN_BYTES, MIX_BYTES) + 64
    uni = sb("uni", [128, UB // 4], F32)

    def carve(layout):
        out = {}
        off = 0
        for name, n, dt in layout:
            nb = ((n * sz[dt] + 31) // 32) * 32
            a = uni[:, off // 4:(off + nb) // 4]
            if dt == BF16:
                a = a.bitcast(BF16)
            out[name] = a[:, 0:n]
            off += nb
        return out

    FM = carve([("aT", FC * GF, BF16), ("sg0", GF, F32), ("sg1", GF, F32)])
    SM = carve([("lmv", DEPTH * 256, F32), ("cmf", 448, F32)])
    lmv, cmf = SM["lmv"], SM["cmf"]
    MM = carve(MIX_LAYOUT)
    aT = FM["aT"].rearrange("p (c g) -> p c g", c=FC)
    sgs = [FM["sg0"], FM["sg1"]]
    KT = MM["KT"].rearrange("p (h t) -> p h t", h=NHEAD)
    VA = MM["VA"].rearrange("p (b h e) -> p b h e", b=NBLK, h=NHEAD)
    catT = MM["catT"].rearrange("p (c g) -> p c g", c=KC)
    QT = MM["QT"].rearrange("p (h g) -> p h g", h=NHEAD)
    df = MM["df"].rearrange("p (i g) -> p i g", i=2)
    uh = MM["uh"].rearrange("p (c x) -> p c x", c=8)

    psb = [es.enter_context(nc.psum_tensor(f"ps{i}", [128, 512], F32)) for i in range(8)]

    wring = Ring(S, wbufs, dma=True)
    wstate = {"n": 0}

    def wreq(src, n):
        idx, buf, w = wring.get()
        t = S.dma("pool", lambda e, buf=buf, src=src, n=n: e.dma_start(out=buf[:, 0:n], in_=src),
                  wring.vs[idx], w)
        return idx, buf, t

    hNr = Ring(S, hN, dma=True)
    hRr = Ring(S, hR, dma=True)
    otr = Ring(S, ot, dma=True)
    misc_vs = VSem(S, 16)

    def sp_load(ring, src, n, extra=()):
        idx, buf, w = ring.get()
        t = S.dma("sp", lambda e, buf=buf, src=src, n=n: e.dma_start(out=buf[:, 0:n], in_=src),
                  ring.vs[idx], list(w) + list(extra))
        return idx, buf, t

    class PBanks:
        def __init__(self, ids):
            self.ids = ids
            self.free = {i: [] for i in ids}
            self.i = 0
            self.held = set()

        def get(self):
            b = self.ids[self.i % len(self.ids)]
            self.i += 1
            assert b not in self.held, f"PSUM bank {b} re-acquired before release"
            self.held.add(b)
            w = self.free[b]
            self.free[b] = []
            return b, w

        def release(self, b, tok):
            assert tok is not None
            self.held.discard(b)
            self.free[b].append(tok)

    def mm_group(out_ap, pairs, waits):
        n = len(pairs)
        t = None
        for i, (l, r) in enumerate(pairs):
            t = S.op("pe", lambda e, l=l, r=r, i=i: e.matmul(out_ap, l, r, start=(i == 0), stop=(i == n - 1)),
                     waits if i == 0 else (), tok=(i == n - 1))
        return t

    state = {"xn_rd": None, "last_store": []}

    t_cv = S.dma("sp", lambda e: e.dma_start(out=cv[:], in_=cvec), misc_vs)
    t_gs = S.dma("sp", lambda e: e.dma_start(out=gsb[:], in_=gsub), VSem(S, 16))
    t_lm = S.dma("sp", lambda e: e.dma_start(out=lmv[:], in_=lamv), VSem(S, 16))
    t_cm = S.dma("sp", lambda e: e.dma_start(out=cmf[:], in_=cmat), VSem(S, 16))
    t_one = S.op("dve", lambda e: e.memset(ones_bf[:], 1.0))
    t_id = S.op("dve", lambda e: e.tensor_copy(out=ident[:], in_=cmf[:, 0:128]), [t_cm])
    t_tri = S.op("dve", lambda e: e.tensor_copy(out=tri[:], in_=cmf[:, 128:256]), [t_cm])
    t_rm = S.op("dve", lambda e: e.tensor_copy(out=rmat[:], in_=cmf[:, 320:448]), [t_cm])
    t_inv = S.op("dve", lambda e: e.tensor_copy(out=invc_t[:], in_=cmf[:, 256:320]), [t_cm])
    invc = invc_t[:, :]
    t_setup = [t_cv, t_one, t_id, t_tri, t_inv, t_rm]
    tc = None
    for l in range(DEPTH):
        lam_init = 0.8 - 0.6 * math.exp(-0.3 * l)
        b = l * 256
        ta = S.op("dve", lambda e, b=b: e.tensor_tensor(out=lamtmp[:], in0=lmv[:, b:b + 64], in1=lmv[:, b + 64:b + 128], op=ALU.mult), [t_lm, tc])
        ta = S.op("dve", lambda e: e.reduce_sum(out=lams[:, 0:1], in_=lamtmp[:], axis=AX.X), [ta])
        tb = S.op("dve", lambda e, b=b: e.tensor_tensor(out=lamtmp[:], in0=lmv[:, b + 128:b + 192], in1=lmv[:, b + 192:b + 256], op=ALU.mult), [ta])
        tb = S.op("dve", lambda e: e.reduce_sum(out=lams[:, 1:2], in_=lamtmp[:], axis=AX.X), [tb])
        te = S.op("act", lambda e: e.activation(out=lams[:, 2:4], in_=lams[:, 0:2], func=AF.Exp), [tb])
        tc = S.op("dve", lambda e: e.tensor_tensor(out=lams[:, 0:1], in0=lams[:, 3:4], in1=lams[:, 2:3], op=ALU.subtract), [te])
        tc = S.op("dve", lambda e, l=l, li=lam_init: e.tensor_scalar(out=neglam[:, l:l + 1], in0=lams[:, 0:1], scalar1=-li, scalar2=None, op0=ALU.add), [tc])
        tg = S.op("dve", lambda e, l=l, li=lam_init: e.tensor_scalar(out=gsb[:, l * 128:(l + 1) * 128], in0=gsb[:, l * 128:(l + 1) * 128], scalar1=1.0 - li, scalar2=None, op0=ALU.mult), [t_gs])
        t_setup += [tc, tg]

    def barrier():
        toks = list(state["last_store"]) + list(t_setup)
        state["last_store"] = []
        for e in ("sp", "act", "dve", "pe"):
            S.op(e, (lambda en: (lambda eng: eng.nop()))(e), toks, tok=False)

    def norm_steps(src, c0, G, gcol, banks, final_dst=None, res=None):
        subs = split_sub(G)
        tacc = None
        for k in range(KC):
            i, buf, tl = sp_load(hNr, src[k, :, c0:c0 + G], G)
            if k == 0:
                tacc = S.op("act", lambda e, buf=buf: e.activation(out=acc[:, 0:G], in_=buf[:, 0:G], func=AF.Square), [tl, state.get("acc_rd")])
                hNr.release(i, tacc)
            else:
                ts = S.op("act", lambda e, buf=buf: e.activation(out=sqf[:, 0:G], in_=buf[:, 0:G], func=AF.Square), [tl, state.get("sq_rd")])
                hNr.release(i, ts)
                tacc = S.op("dve", lambda e: e.tensor_tensor(out=acc[:, 0:G], in0=acc[:, 0:G], in1=sqf[:, 0:G], op=ALU.add), [ts, tacc])
                state["sq_rd"] = tacc
            yield
        th = S.op("dve", lambda e: e.tensor_copy(out=hi_t[:, 0:G], in_=acc[:, 0:G]), [tacc, state.get("hilo_rd")])
        t2 = S.op("dve", lambda e: e.tensor_tensor(out=acc[:, 0:G], in0=acc[:, 0:G], in1=hi_t[:, 0:G], op=ALU.subtract), [th])
        tlo = S.op("dve", lambda e: e.tensor_copy(out=lo_t[:, 0:G], in_=acc[:, 0:G]), [t2])
        state["acc_rd"] = tlo
        tr = None
        for s, (off, n) in enumerate(subs):
            bk, w = banks.get()
            tp = mm_group(psb[bk][:, 0:n], [(ones_bf[:, :], hi_t[:, off:off + n]), (ones_bf[:, :], lo_t[:, off:off + n])], [th, tlo] + w)
            state["hilo_rd"] = tp
            t1 = S.op("act", lambda e, bk=bk, off=off, n=n: e.activation(out=rstd[:, off:off + n], in_=psb[bk][:, 0:n], func=AF.Ln, scale=1.0 / D, bias=eps_ap), [tp, state.get("rstd_rd")])
            banks.release(bk, t1)
            tr = S.op("act", lambda e, off=off, n=n: e.activation(out=rstd[:, off:off + n], in_=rstd[:, off:off + n], func=AF.Exp, scale=-0.5), [t1])
        yield
        outs = []
        for k in range(KC):
            i, buf, tl = sp_load(hNr, src[k, :, c0:c0 + G], G)
            if final_dst is None:
                tx = S.op("dve", lambda e, buf=buf, k=k: e.scalar_tensor_tensor(
                    out=xn[:, k, 0:G], in0=buf[:, 0:G], scalar=cv[:, gcol + k:gcol + k + 1], in1=rstd[:, 0:G],
                    op0=ALU.mult, op1=ALU.mult), [tl, tr, state["xn_rd"]])
                hNr.release(i, tx)
                outs.append(tx)
                state["rstd_rd"] = tx
            else:
                oi, obuf, ow = otr.get()
                tx = S.op("dve", lambda e, buf=buf, obuf=obuf, k=k: e.scalar_tensor_tensor(
                    out=obuf[:, 0:G], in0=buf[:, 0:G], scalar=cv[:, gcol + k:gcol + k + 1], in1=rstd[:, 0:G],
                    op0=ALU.mult, op1=ALU.mult), [tl, tr] + ow)
                hNr.release(i, tx)
                state["rstd_rd"] = tx
                tst = S.dma("sp", lambda e, obuf=obuf, k=k: e.dma_start(out=final_dst[k, :, c0:c0 + G], in_=obuf[:, 0:G]), otr.vs[oi], [tx])
                otr.release(oi, tst)
                state["last_store"].append(tst)
            yield
        if res is not None:
            res["xtok"] = outs

    def norm_pass(src, c0, G, gcol, banks, final_dst=None):
        res = {}
        for _ in norm_steps(src, c0, G, gcol, banks, final_dst, res):
            pass
        return res["xtok"]

    def drive(gen, st, j):
        if gen is None:
            return
        if j < 6:
            target = min(16, 3 * (j + 1))
        elif j < 10:
            target = 16
        elif j == 10:
            target = 17
        else:
            target = 17 + 4 * (j - 10)
        while st["n"] < target:
            try:
                next(gen)
            except StopIteration:
                st["n"] = 10 ** 6
                return
            st["n"] += 1

    def advance(gen, n):
        if gen is None:
            return
        for _ in range(n):
            try:
                next(gen)
            except StopIteration:
                return

    eps_t = sb("eps_t", [128, 2], F32)
    S.op("dve", lambda e: e.memset(eps_t[:, 0:1], NORM_EPS), tok=False)
    t_eps = S.op("dve", lambda e: e.memset(eps_t[:, 1:2], SUBLN_EPS))
    t_setup.append(t_eps)
    eps_ap = eps_t[:, 0:1]
    eps2_ap = eps_t[:, 1:2]

    def ffn_phase(l, f, src, dst):
        banks = PBanks(list(range(8)))
        gcol = (l * 3 + (0 if f == 0 else 2)) * KC
        gu_base = ((l * 2 + f) * 2) * FC
        dn_base = (l * 2 + f) * KC * 4
        sgr = Ring(S, sgs)
        groups = cfg.ffn_groups
        nres = {}
        advance(norm_steps(src, groups[0][0], groups[0][1], gcol, banks, None, nres), 1000)
        for gi_, (c0, G) in enumerate(groups):
            subs = split_sub(G)
            xtok = nres["xtok"]
            a_tok = []
            last_gu = None
            for c in range(FC):
                gi, gb, tg = wreq(w_gu[gu_base + c], WT)
                ui, ubf, tu = wreq(w_gu[gu_base + FC + c], WT)
                gbk = []
                ubk = []
                for wb, tw, lst in ((gb, tg, gbk), (ubf, tu, ubk)):
                    for s, (off, n) in enumerate(subs):
                        bk, w = banks.get()
                        t = mm_group(psb[bk][:, 0:n],
                                     [(wb[:, k * 128:(k + 1) * 128], xn[:, k, off:off + n]) for k in range(KC)],
                                     [tw] + w + (xtok if c == 0 else []))
                        lst.append((bk, t))
                last_gu = ubk[-1][1]
                wring.release(gi, gbk[-1][1])
                wring.release(ui, last_gu)
                si, sgb, sw = sgr.get()
                for s, (off, n) in enumerate(subs):
                    bk, t = gbk[s]
                    t1 = S.op("act", lambda e, sgb=sgb, bk=bk, off=off, n=n: e.activation(out=sgb[:, off:off + n], in_=psb[bk][:, 0:n], func=AF.Silu), [t] + sw)
                    banks.release(bk, t1)
                    bk2, t2 = ubk[s]
                    t3 = S.op("dve", lambda e, sgb=sgb, bk2=bk2, off=off, n=n, c=c: e.tensor_tensor(out=aT[:, c, off:off + n], in0=psb[bk2][:, 0:n], in1=sgb[:, off:off + n], op=ALU.mult), [t1, t2])
                    banks.release(bk2, t3)
                sgr.release(si, t3)
                a_tok.append(t3)
            state["xn_rd"] = last_gu
            ngen = None
            nst = {"n": 0}
            if gi_ + 1 < len(groups):
                nres = {}
                ngen = norm_steps(src, groups[gi_ + 1][0], groups[gi_ + 1][1], gcol, banks, None, nres)
            pre = {}
            for j in range(min(2, KC)):
                pre[j] = sp_load(hRr, src[j, :, c0:c0 + G], G)
            for j in range(KC):
                tiles = []
                for qd in range(4):
                    tiles.append(wreq(w_dn[dn_base + j * 4 + qd], 11 * 128))
                bks = []
                for s, (off, n) in enumerate(subs):
                    bk, w = banks.get()
                    pairs = []
                    for c in range(FC):
                        wb = tiles[c // 11][1]
                        cc = c % 11
                        pairs.append((wb[:, cc * 128:(cc + 1) * 128], aT[:, c, off:off + n]))
                    t = mm_group(psb[bk][:, 0:n], pairs, [tt[2] for tt in tiles] + w + (a_tok if j == 0 else []))
                    bks.append((bk, t))
                for tt in tiles:
                    wring.release(tt[0], bks[-1][1])
                ri, rbuf, rt = pre.pop(j)
                oi, obuf, ow = otr.get()
                for s, (off, n) in enumerate(subs):
                    bk, t = bks[s]
                    te = S.op("dve", lambda e, obuf=obuf, rbuf=rbuf, bk=bk, off=off, n=n: e.scalar_tensor_tensor(
                        out=obuf[:, off:off + n], in0=psb[bk][:, 0:n], scalar=0.5, in1=rbuf[:, off:off + n],
                        op0=ALU.mult, op1=ALU.add), [t, rt] + ow)
                    banks.release(bk, te)
                hRr.release(ri, te)
                if j + 2 < KC:
                    pre[j + 2] = sp_load(hRr, src[j + 2, :, c0:c0 + G], G)
                tst = S.dma("sp", lambda e, obuf=obuf, j=j: e.dma_start(out=dst[j, :, c0:c0 + G], in_=obuf[:, 0:G]), otr.vs[oi], [te])
                otr.release(oi, tst)
                state["last_store"].append(tst)
                drive(ngen, nst, j)
            advance(ngen, 1000)

    Er = Ring(S, [MM["E0"], MM["E1"], MM["E2"], MM["E3"]])
    o_r = Ring(S, [MM["o0"], MM["o1"]])
    on_r = Ring(S, [MM["on0"], MM["on1"]])
    rc_r = Ring(S, [MM["rc0"], MM["rc1"]])
    vT_r = Ring(S, [MM["vT0"], MM["vT1"]])

    def mixer_phase(l):
        pbanks = PBanks([0, 1, 2])
        sbanks = PBanks([3, 4, 5])
        obanks = PBanks([6, 7])
        qb_r = Ring(S, [MM["qb0"], MM["qb1"]])
        gcol = (l * 3 + 1) * KC
        pscol = (3 * DEPTH + 1) * KC + l * 8
        cos_vs, sin_vs = VSem(S, 16), VSem(S, 16)
        wl_vs = VSem(S, 16)
        rope_last = [None]
        t_va = None
        allg = [(s_, t_, g_) for s_ in range(cfg.nseq) for (t_, g_) in cfg.mix_groups]
        nres = {}
        advance(norm_steps(hT, allg[0][0] * L + allg[0][1], allg[0][2], gcol, pbanks, None, nres), 1000)
        pend_t = []

        def flush_t():
            while pend_t:
                pend_t.pop(0)()

        for sq_i in range(cfg.nseq):
            t_va = S.op("dve", lambda e: e.memset(MM["VA"], 1.0))
            t_uh = S.op("dve", lambda e: e.memset(MM["uh"], 0.0))
            for gidx, (t0, G) in enumerate(cfg.mix_groups):
                c0 = sq_i * L + t0
                gflat = sq_i * len(cfg.mix_groups) + gidx
                subs = split_sub(G)
                blocks = []
                o = 0
                while o < G:
                    nq = min(128, G - o)
                    blocks.append(((t0 + o) // 128, o, nq))
                    o += nq
                xtok = nres["xtok"]
                tcs = S.dma("sp", lambda e, t0=t0, G=G: e.dma_start(out=MM["cosg"][:, 0:G], in_=ropec[:, t0:t0 + G]), cos_vs, [rope_last[0]])
                tsn = S.dma("sp", lambda e, t0=t0, G=G: e.dma_start(out=MM["sing"][:, 0:G], in_=ropes[:, t0:t0 + G]), sin_vs, [rope_last[0]])
                first_proj = [True]

                def proj(widx):
                    wi, wb, tw = wreq(w_in[l * 32 + widx], WT)
                    res = []
                    for s, (off, n) in enumerate(subs):
                        bk, w = pbanks.get()
                        t = mm_group(psb[bk][:, 0:n],
                                     [(wb[:, k * 128:(k + 1) * 128], xn[:, k, off:off + n]) for k in range(KC)],
                                     [tw] + w + (xtok if first_proj[0] else []))
                        first_proj[0] = False
                        res.append((bk, t))
                    wring.release(wi, res[-1][1])
                    state["xn_rd"] = res[-1][1]
                    return res

                def rope1(pa, t1buf):
                    qi, qbuf, qw = qb_r.get()
                    ta_ = None
                    for s, (off, n) in enumerate(subs):
                        b1, ta = pa[s]
                        tc_ = S.op("act", lambda e, b1=b1, off=off, n=n: e.activation(out=qbuf[:, off:off + n], in_=psb[b1][:, 0:n], func=AF.Copy), [ta] + qw)
                        ta_ = S.op("dve", lambda e, b1=b1, off=off, n=n: e.tensor_tensor(out=t1buf[:, off:off + n], in0=psb[b1][:, 0:n], in1=MM["cosg"][:, off:off + n], op=ALU.mult), [tc_, tcs, rope_last[0]])
                        pbanks.release(b1, ta_)
                    return qi, qbuf, tc_, ta_

                def rope2(r1, t1buf, dst_fn):
                    qi, qbuf, tc_, ta_ = r1
                    tl = None
                    tr_ = None
                    for s, (off, n) in enumerate(subs):
                        bk, w = pbanks.get()
                        tr_ = S.op("pe", lambda e, bk=bk, off=off, n=n: e.matmul(psb[bk][:, 0:n], rmat[:, :], qbuf[:, off:off + n], start=True, stop=True), [tc_] + w)
                        x2 = S.op("dve", lambda e, bk=bk, off=off, n=n: e.tensor_tensor(out=MM["t3"][:, off:off + n], in0=psb[bk][:, 0:n], in1=MM["sing"][:, off:off + n], op=ALU.mult), [tr_, tsn, rope_last[0]])
                        pbanks.release(bk, x2)
                        tl = S.op("dve", lambda e, off=off, n=n: e.tensor_tensor(out=dst_fn(off, n), in0=t1buf[:, off:off + n], in1=MM["t3"][:, off:off + n], op=ALU.add), [x2, ta_])
                        rope_last[0] = tl
                    qb_r.release(qi, tr_)
                    return tl

                def pool_gen():
                    cat_tok = state["cat_tok"]
                    ub, ua, uc = MM["ub"], MM["ua"], MM["uc"]
                    pool_last = state.get("pool_last")
                    for g in range(4):
                        w = WINDOWS[g]
                        dtoks = []
                        for ic in range(2):
                            ucx = 2 * g + ic
                            pu = proj(24 + ucx)
                            th = S.op("dve", lambda e, ucx=ucx: e.tensor_copy(out=ub[:, 0:16], in_=uh[:, ucx, :]), [t_uh, pool_last])
                            tu = None
                            for s, (off, n) in enumerate(subs):
                                bk, t = pu[s]
                                tu = S.op("act", lambda e, bk=bk, off=off, n=n: e.activation(out=ub[:, 16 + off:16 + off + n], in_=psb[bk][:, 0:n], func=AF.Copy), [t, pool_last])
                                pbanks.release(bk, tu)
                            th2 = S.op("dve", lambda e, ucx=ucx, G=G: e.tensor_copy(out=uh[:, ucx, :], in_=ub[:, G:G + 16]), [tu, th])
                            srcb, lvl, tprev = ub, 1, th2
                            tmps = [ua, uc]
                            ti = 0
                            while lvl < w:
                                dstb = tmps[ti % 2]
                                ti += 1
                                lo = 2 * lvl - 1
                                tprev = S.op("dve", lambda e, srcb=srcb, dstb=dstb, lo=lo, lvl=lvl, G=G: e.tensor_tensor(
                                    out=dstb[:, lo:16 + G], in0=srcb[:, lo:16 + G], in1=srcb[:, lo - lvl:16 + G - lvl], op=ALU.add), [tprev])
                                srcb = dstb
                                lvl *= 2
                            td = S.op("dve", lambda e, srcb=srcb, ic=ic, w=w, G=G: e.scalar_tensor_tensor(
                                out=df[:, ic, 0:G], in0=srcb[:, 16:16 + G], scalar=1.0 / w, in1=ub[:, 16:16 + G],
                                op0=ALU.mult, op1=ALU.subtract), [tprev, state.get("df_rd")])
                            if t0 == 0:
                                other = tmps[ti % 2]
                                tf = S.op("dve", lambda e, srcb=srcb, other=other, g=g: e.tensor_tensor(out=other[:, 0:16], in0=srcb[:, 16:32], in1=invc[:, g * 16:(g + 1) * 16], op=ALU.mult), [td])
                                td = S.op("dve", lambda e, other=other, ic=ic: e.tensor_tensor(out=df[:, ic, 0:16], in0=other[:, 0:16], in1=ub[:, 16:32], op=ALU.subtract), [tf])
                            pool_last = td
                            dtoks.append(td)
                            yield
                        wi_p, wb_p, tw_p = wreq(w_pl[l], WT)
                        for oc in range(2):
                            res = []
                            for s, (off, n) in enumerate(subs):
                                bk, wv = pbanks.get()
                                pairs = []
                                for ic in range(2):
                                    col = ((g * 2 + ic) * 2 + oc) * 128
                                    pairs.append((wb_p[:, col:col + 128], df[:, ic, off:off + n]))
                                t = mm_group(psb[bk][:, 0:n], pairs, [tw_p] + wv + dtoks)
                                res.append((bk, t))
                            state["df_rd"] = res[-1][1]
                            cidx = 8 + 2 * g + oc
                            for s, (off, n) in enumerate(subs):
                                bk, t = res[s]
                                tcp = S.op("dve", lambda e, bk=bk, off=off, n=n, cidx=cidx: e.tensor_scalar(
                                    out=catT[:, cidx, off:off + n], in0=psb[bk][:, 0:n], scalar1=cv[:, pscol + cidx - 8:pscol + cidx - 7],
                                    scalar2=None, op0=ALU.mult), [t, state["cat_rd"]])
                                pbanks.release(bk, tcp)
                                cat_tok.append(tcp)
                        wring.release(wi_p, state["df_rd"])
                        yield
                    state["pool_last"] = pool_last
                pgen = pool_gen()
                for h in range(NHEAD):
                    pq = proj(h)
                    r1q = rope1(pq, MM["t1"])
                    pk = proj(8 + h)
                    r1k = rope1(pk, MM["t2"])
                    tq = rope2(r1q, MM["t1"], lambda off, n, h=h: QT[:, h, off:off + n])
                    tk = rope2(r1k, MM["t2"], lambda off, n, h=h: KT[:, h, t0 + off:t0 + off + n])
                    flush_t()
                    pv = proj(16 + h)
                    vi, vbuf, vw = vT_r.get()
                    tv = None
                    for s, (off, n) in enumerate(subs):
                        bk, t = pv[s]
                        tv = S.op("act", lambda e, vbuf=vbuf, bk=bk, off=off, n=n: e.activation(out=vbuf[:, off:off + n], in_=psb[bk][:, 0:n], func=AF.Copy), [t] + vw)
                        pbanks.release(bk, tv)
                    tvl = []
                    tlast = None
                    tb_id, w = pbanks.get()
                    tbank = psb[tb_id][:, :].bitcast(BF16)
                    for sl, (gb, o, nq) in enumerate(blocks):
                        tlast = S.op("pe", lambda e, sl=sl, vbuf=vbuf, o=o, nq=nq: e.transpose(tbank[0:nq, sl * 128:(sl + 1) * 128], vbuf[:, o:o + nq], ident[:, :]), [tv] + w)
                    for sl, (gb, o, nq) in enumerate(blocks):
                        tcp = S.op("act", lambda e, sl=sl, gb=gb, nq=nq, h=h: e.activation(out=VA[0:nq, gb, h, 0:128], in_=tbank[0:nq, sl * 128:(sl + 1) * 128], func=AF.Copy), [tlast, t_va])
                        tvl.append(tcp)
                    pbanks.release(tb_id, tcp)
                    vT_r.release(vi, tlast)
                    chunks = []
                    for (gb, o, nq) in blocks:
                        for m in range(2):
                            kbs = list(range(gb + 1))
                            for ci in range(0, len(kbs), 4):
                                ch = kbs[ci:ci + 4]
                                if len(ch) > 1 and (ch[-1] + 1) * 128 > L:
                                    chunks.append((gb, o, nq, m, ch[:-1]))
                                    chunks.append((gb, o, nq, m, ch[-1:]))
                                else:
                                    chunks.append((gb, o, nq, m, ch))
                    cur_o = {}
                    sinfo = [None] * len(chunks)

                    def emit_S(ci):
                        gb, o, nq, m, ch = chunks[ci]
                        nk = min(128, L - ch[0] * 128)
                        bk, w = sbanks.get()
                        t = None
                        for i, kb in enumerate(ch):
                            t = S.op("pe", lambda e, bk=bk, i=i, kb=kb, nk=nk, nq=nq, m=m, o=o: e.matmul(
                                psb[bk][0:nk, i * nq:(i + 1) * nq],
                                KT[m * 64:(m + 1) * 64, h, kb * 128:kb * 128 + nk],
                                QT[m * 64:(m + 1) * 64, h, o:o + nq], start=True, stop=True),
                                ([tq, tk] + w) if i == 0 else (), tok=(i == len(ch) - 1))
                        ei, eb, ew = Er.get()
                        ncol = len(ch) * nq
                        te = S.op("act", lambda e, eb=eb, bk=bk, nk=nk, ncol=ncol: e.activation(out=eb[0:nk, 0:ncol], in_=psb[bk][0:nk, 0:ncol], func=AF.Exp, scale=0.125), [t] + ew)
                        sbanks.release(bk, te)
                        if ch[-1] == gb:
                            i = len(ch) - 1
                            te = S.op("dve", lambda e, eb=eb, i=i, nk=nk, nq=nq: e.tensor_tensor(out=eb[0:nk, i * nq:(i + 1) * nq], in0=eb[0:nk, i * nq:(i + 1) * nq], in1=tri[0:nk, 0:nq], op=ALU.mult), [te])
                        sinfo[ci] = (ei, eb, te, nk)

                    def emit_AV(ci):
                        gb, o, nq, m, ch = chunks[ci]
                        ei, eb, te, nk = sinfo[ci]
                        if (gb, m) == (gb, 0) and ch[0] == 0 and m == 0:
                            ob, w = obanks.get()
                            cur_o[gb] = (ob, w)
                        ob, w0 = cur_o[gb]
                        t = None
                        for i, kb in enumerate(ch):
                            w = [te] + tvl
                            if kb == 0:
                                w = w + w0
                            t = S.op("pe", lambda e, ob=ob, eb=eb, i=i, kb=kb, nk=nk, nq=nq, m=m, gb=gb: e.matmul(
                                psb[ob][0:nq, m * 256:m * 256 + 129],
                                eb[0:nk, i * nq:(i + 1) * nq],
                                VA[0:nk, kb, h, 0:129], start=(kb == 0), stop=(kb == gb)),
                                w if i == 0 or kb == 0 else (), tok=(i == len(ch) - 1))
                        Er.release(ei, t)
                        if m == 1 and ch[-1] == gb:
                            combine(gb, o, nq, ob, t)

                    def combine(gb, o, nq, ob, tav):
                        P = psb[ob]
                        ri, rc, rw = rc_r.get()
                        a1 = S.op("dve", lambda e: e.reciprocal(out=rc[0:nq, 0:1], in_=P[0:nq, 128:129]), [tav] + rw)
                        a2 = S.op("dve", lambda e: e.reciprocal(out=rc[0:nq, 1:2], in_=P[0:nq, 384:385]), [tav])
                        a3 = S.op("dve", lambda e: e.tensor_tensor(out=rc[0:nq, 1:2], in0=rc[0:nq, 1:2], in1=neglam[0:nq, l:l + 1], op=ALU.mult), [a2])
                        oi, ob_, ow = o_r.get()
                        a4 = S.op("dve", lambda e: e.tensor_scalar(out=ob_[0:nq, 0:128], in0=P[0:nq, 0:128], scalar1=rc[0:nq, 0:1], scalar2=None, op0=ALU.mult), [a1] + ow)
                        a5 = S.op("dve", lambda e: e.scalar_tensor_tensor(out=ob_[0:nq, 0:128], in0=P[0:nq, 256:384], scalar=rc[0:nq, 1:2], in1=ob_[0:nq, 0:128], op0=ALU.mult, op1=ALU.add), [a3, a4])
                        obanks.release(ob, a5)
                        a6 = S.op("dve", lambda e: e.tensor_tensor(out=MM["so"][0:nq, 0:128], in0=ob_[0:nq, 0:128], in1=ob_[0:nq, 0:128], op=ALU.mult), [a5, state.get("so_rd")])
                        a7 = S.op("dve", lambda e: e.reduce_sum(out=rc[0:nq, 2:3], in_=MM["so"][0:nq, 0:128], axis=AX.X), [a6])
                        state["so_rd"] = a7
                        a8 = S.op("act", lambda e: e.activation(out=rc[0:nq, 3:4], in_=rc[0:nq, 2:3], func=AF.Ln, scale=1.0 / 128, bias=eps2_ap[0:nq, :]), [a7])
                        a9 = S.op("act", lambda e: e.activation(out=rc[0:nq, 3:4], in_=rc[0:nq, 3:4], func=AF.Exp, scale=-0.5), [a8])
                        ni, nb, nw = on_r.get()
                        a10 = S.op("dve", lambda e: e.scalar_tensor_tensor(out=nb[0:nq, 0:128], in0=ob_[0:nq, 0:128], scalar=rc[0:nq, 3:4], in1=gsb[0:nq, l * 128:(l + 1) * 128], op0=ALU.mult, op1=ALU.mult), [a9] + nw)
                        o_r.release(oi, a10)
                        rc_r.release(ri, a10)
                        def later(h=h, o=o, nq=nq, nb=nb, ni=ni, a10=a10, cat_tok=cat_tok):
                            tb_id, w = pbanks.get()
                            tbank = psb[tb_id][:, :].bitcast(BF16)
                            a11 = S.op("pe", lambda e: e.transpose(tbank[0:128, 0:nq], nb[0:nq, 0:128], ident[0:nq, 0:nq]), [a10] + w)
                            on_r.release(ni, a11)
                            a12 = S.op("act", lambda e: e.activation(out=catT[:, h, o:o + nq], in_=tbank[0:128, 0:nq], func=AF.Copy), [a11, state["cat_rd"]])
                            pbanks.release(tb_id, a12)
                            cat_tok.append(a12)
                        flush_t()
                        pend_t.append(later)

                    if h == 0:
                        cat_tok = []
                        state["cat_tok"] = cat_tok
                    else:
                        cat_tok = state["cat_tok"]
                    emit_S(0)
                    if len(chunks) > 1:
                        emit_S(1)
                    for ci in range(len(chunks)):
                        if ci + 2 < len(chunks):
                            emit_S(ci + 2)
                        emit_AV(ci)
                    advance(pgen, 1 if h % 2 == 0 else 2)

                flush_t()
                advance(pgen, 1000)
                ngen = None
                nst = {"n": 0}
                if gflat + 1 < len(allg):
                    nres = {}
                    ns_, nt_, ng_ = allg[gflat + 1]
                    ngen = norm_steps(hT, ns_ * L + nt_, ng_, gcol, pbanks, None, nres)
                pre = {}
                for j in range(2):
                    pre[j] = sp_load(hRr, hT[j, :, c0:c0 + G], G)
                last_o = None
                for j in range(KC):
                    wi, wb, tw = wreq(w_ot[l * KC + j], WT)
                    bks = []
                    for s, (off, n) in enumerate(subs):
                        bk, wv = pbanks.get()
                        t = mm_group(psb[bk][:, 0:n],
                                     [(wb[:, c * 128:(c + 1) * 128], catT[:, c, off:off + n]) for c in range(KC)],
                                     [tw] + wv + (cat_tok if j == 0 else []))
                        bks.append((bk, t))
                    wring.release(wi, bks[-1][1])
                    last_o = bks[-1][1]
                    ri, rbuf, rt = pre.pop(j)
                    oi, obuf, ow = otr.get()
                    te = None
                    for s, (off, n) in enumerate(subs):
                        bk, t = bks[s]
                        te = S.op("dve", lambda e, obuf=obuf, rbuf=rbuf, bk=bk, off=off, n=n: e.tensor_tensor(
                            out=obuf[:, off:off + n], in0=psb[bk][:, 0:n], in1=rbuf[:, off:off + n], op=ALU.add), [t, rt] + ow)
                        pbanks.release(bk, te)
                    hRr.release(ri, te)
                    if j + 2 < KC:
                        pre[j + 2] = sp_load(hRr, hT[j + 2, :, c0:c0 + G], G)
                    tst = S.dma("sp", lambda e, obuf=obuf, j=j, c0=c0, G=G: e.dma_start(out=hT[j, :, c0:c0 + G], in_=obuf[:, 0:G]), otr.vs[oi], [te])
                    otr.release(oi, tst)
                    state["last_store"].append(tst)
                    drive(ngen, nst, j)
                advance(ngen, 1000)
                state["cat_rd"] = last_o

    state["cat_rd"] = None

    barrier()
    for l in range(DEPTH):
        ffn_phase(l, 0, xT if l == 0 else hT, hT)
        barrier()
        mixer_phase(l)
        barrier()
        ffn_phase(l, 1, hT, hT)
        barrier()
    fb = PBanks(list(range(8)))
    for (c0, G) in cfg.ffn_groups:
        norm_pass(hT, c0, G, 3 * DEPTH * KC, fb, final_dst=outT)
    S.op("sp", lambda e: e.nop(), state["last_store"], tok=False)

    import os
    if os.environ.get("K_DEBUG"):
        print("nsem", S.nsem, {k: len(v) for k, v in S.q.items()}, {k: (v.cnt) for k, v in S.vs.items()}, flush=True)
    with nc.Block() as block:
        S.emit(block)
    es.close()
    return nc


def _tile_w(W, nk, no):
    return np.ascontiguousarray(W.reshape(nk, 128, no, 128).transpose(2, 1, 0, 3)).reshape(no, 128, nk * 128)


def prep_weights(inp, depth):
    gu = np.empty((depth, 2, 2, FC, 128, WT), np.float32)
    dn = np.empty((depth, 2, KC, 4, 128, 11 * 128), np.float32)
    win = np.empty((depth, 32, 128, WT), np.float32)
    wpl = np.empty((depth, 128, WT), np.float32)
    wot = np.empty((depth, KC, 128, WT), np.float32)
    perm = np.arange(1024).reshape(16, 64)
    perm = np.concatenate([perm[:, 32:], perm[:, :32]], axis=1).reshape(-1)
    for l in range(depth):
        for f, pre in enumerate(("ffn1", "ffn2")):
            gu[l, f, 0] = _tile_w(np.asarray(inp[pre + "_w_gate"][l]), KC, FC)
            gu[l, f, 1] = _tile_w(np.asarray(inp[pre + "_w_up"][l]), KC, FC)
            d = _tile_w(np.asarray(inp[pre + "_w_down"][l]), FC, KC)
            dn[l, f] = d.reshape(KC, 128, 4, 11 * 128).transpose(0, 2, 1, 3)
        W = np.asarray(inp["w_in"][l])
        q, k, v, u = W[:, :1024], W[:, 1024:2048], W[:, 2048:3072], W[:, 3072:]
        win[l] = _tile_w(W, KC, 32)
        wp = np.asarray(inp["w_pool"][l])
        wpl[l] = wp.reshape(4, 2, 128, 2, 128).transpose(2, 0, 1, 3, 4).reshape(128, WT)
        wot[l] = _tile_w(np.asarray(inp["w_out"][l]), KC, KC)
    return {
        "w_gu": gu.reshape(-1, 128, WT), "w_dn": dn.reshape(-1, 128, 11 * 128),
        "w_in": win.reshape(-1, 128, WT), "w_pl": wpl, "w_ot": wot.reshape(-1, 128, WT),
    }


def prep_consts(inp, depth, L):
    def col(v):
        return np.asarray(v, np.float32).reshape(-1, 128).T
    cols = []
    for l in range(depth):
        cols += [col(inp["ffn1_norm_g"][l]), col(inp["mix_norm_g"][l]), col(inp["ffn2_norm_g"][l])]
    cols.append(col(inp["final_norm_g"]))
    for l in range(depth):
        cols.append(col(inp["pool_scale"][l]))
    cvec = np.ascontiguousarray(np.concatenate(cols, axis=1), np.float32)
    gsub = np.ascontiguousarray(np.broadcast_to(np.asarray(inp["subln_g"], np.float32)[:depth].reshape(1, -1), (128, depth * 128)))
    lam = np.stack([np.asarray(inp[k], np.float32)[:depth] for k in ("lam_q1", "lam_k1", "lam_q2", "lam_k2")], axis=1)
    lamv = np.ascontiguousarray(np.broadcast_to(lam.reshape(1, -1), (128, depth * 256)))
    cmat = np.zeros((128, 448), np.float32)
    cmat[:, 0:128] = np.eye(128, dtype=np.float32)
    kk = np.arange(128)
    cmat[:, 128:256] = (kk[None, :] >= kk[:, None]).astype(np.float32)
    for po in range(128):
        d = po % 64
        if d < 32:
            cmat[po + 32, 320 + po] = -1.0
        else:
            cmat[po - 32, 320 + po] = 1.0
    for g, w in enumerate(WINDOWS):
        t = np.arange(16)
        cmat[:, 256 + g * 16:256 + (g + 1) * 16] = (1.0 / np.minimum(t + 1, w)).astype(np.float32)[None, :]
    pos = np.arange(L, dtype=np.float32)
    inv_freq = (np.float32(1.0) / (np.float32(10000.0) ** (np.arange(0, 64, 2, dtype=np.float32) / np.float32(64)))).astype(np.float32)
    ang = (pos[:, None] * inv_freq[None, :]).astype(np.float32)
    c = np.cos(ang.astype(np.float64)).astype(np.float32).T
    s = np.sin(ang.astype(np.float64)).astype(np.float32).T
    ropec = np.ascontiguousarray(np.concatenate([c, c, c, c], axis=0))
    ropes = np.ascontiguousarray(np.concatenate([s, s, s, s], axis=0))
    return {"cvec": cvec, "gsub": gsub, "lamv": lamv, "cmat": cmat, "ropec": ropec, "ropes": ropes}


def run(inputs, cfg, ncores, trace=False):
    x = np.asarray(inputs["x"], np.float32)
    meta = np.asarray(inputs["meta_tokens"], np.float32)
    B, SEQ, _ = x.shape
    assert B == ncores * cfg.nseq and SEQ + NMETA == cfg.L
    shared = {}
    shared.update(prep_weights(inputs, cfg.depth))
    shared.update(prep_consts(inputs, cfg.depth, cfg.L))
    in_maps = []
    for c in range(ncores):
        cols = []
        for s in range(cfg.nseq):
            hb = np.concatenate([meta, x[c * cfg.nseq + s]], axis=0)
            cols.append(hb.T)
        xt = np.ascontiguousarray(np.concatenate(cols, axis=1)).reshape(KC, 128, cfg.NT)
        m = dict(shared)
        m["xT"] = xt
        in_maps.append(m)
    nc = build_program(cfg)
    res = run_bass_kernel_spmd(nc, in_maps, core_ids=list(range(ncores)), trace=trace)
    out = np.empty((B, SEQ, D), np.float32)
    for c in range(ncores):
        o = np.asarray(res.results[c]["outT"]).reshape(D, cfg.NT)
        for s in range(cfg.nseq):
            out[c * cfg.nseq + s] = o[:, s * cfg.L + NMETA:(s + 1) * cfg.L].T
    return out, res


def kernel(**inputs):
    cfg = Cfg(depth=4, nseq=2, seqlen=2048, ffn_g=688)
    out, _ = run(inputs, cfg, 8)
    return out
```

```python
import math
from contextlib import ExitStack

import numpy as np
import concourse.bass as bass
import concourse.mybir as mybir
from concourse.bass_utils import run_bass_kernel_spmd

F32 = mybir.dt.float32
BF16 = mybir.dt.bfloat16
AF = mybir.ActivationFunctionType
ALU = mybir.AluOpType
AX = mybir.AxisListType

D = 2048
DFF = 5632
KC = D // 128
FC = DFF // 128
NMETA = 16
NHEAD = 8
NORM_EPS = 1e-6
SUBLN_EPS = 1e-5
WINDOWS = (2, 4, 8, 16)
WT = 2048


def split_sub(G):
    ns = (G + 511) // 512
    base = (G + ns - 1) // ns
    out = []
    o = 0
    while o < G:
        n = min(base, G - o)
        out.append((o, n))
        o += n
    return out


class Cfg:
    def __init__(self, depth=4, nseq=2, seqlen=2048, ffn_g=688):
        self.depth = depth
        self.nseq = nseq
        self.L = NMETA + seqlen
        self.NT = nseq * self.L
        self.ffn_groups = []
        o = 0
        while o < self.NT:
            g = min(ffn_g, self.NT - o)
            self.ffn_groups.append((o, g))
            o += g
        self.GF = max(g for _, g in self.ffn_groups)
        self.mix_groups = []
        o = 0
        while o < self.L:
            g = min(512, self.L - o)
            if self.L - (o + g) < 128 and self.L - (o + g) > 0:
                g = self.L - o
            self.mix_groups.append((o, g))
            o += g
        self.GM = max(g for _, g in self.mix_groups)
        self.nblk = (self.L + 127) // 128


class VSem:
    LIMIT = 30000

    def __init__(self, S, step):
        self.S = S
        self.step = step
        self.hw = None
        self.cnt = 0

    def bump(self):
        if self.hw is None or self.cnt + self.step > self.LIMIT:
            self.hw = self.S.new_hw()
            self.cnt = 0
        self.cnt += self.step
        return (self.hw, self.cnt)


class _Rec:
    def __getattr__(self, name):
        def f(*a, **k):
            self.call = (name, a, k)
        return f


class Sched:
    ENGS = ("sp", "act", "dve", "pool", "pe")

    def __init__(self, nc, es):
        self.nc = nc
        self.es = es
        self.q = {e: [] for e in self.ENGS}
        self.nsem = 0
        self.vs = {e: VSem(self, 1) for e in ("pe", "act", "dve")}

    def new_hw(self):
        self.nsem += 1
        return self.es.enter_context(self.nc.semaphore(f"sem{self.nsem}"))

    @staticmethod
    def _norm(waits):
        best = {}
        for w in waits:
            if w is None:
                continue
            hw, v = w
            k = id(hw)
            if k not in best or best[k][1] < v:
                best[k] = (hw, v)
        return list(best.values())

    def op(self, eng, fn, waits=(), tok=True):
        r = _Rec()
        fn(r)
        t = self.vs[eng].bump() if tok else None
        self.q[eng].append((r.call, self._norm(waits), t, 1))
        return t

    def dma(self, eng, fn, vsem, waits=()):
        r = _Rec()
        fn(r)
        t = vsem.bump()
        self.q[eng].append((r.call, self._norm(waits), t, 16))
        return t

    def emit(self, block):
        def runner(lst):
            def f(e):
                seen = {}
                for (meth, a, kw), waits, t, step in lst:
                    for hw, v in waits:
                        k = id(hw)
                        if seen.get(k, 0) >= v:
                            continue
                        e.wait_ge(hw, v)
                        seen[k] = v
                    ins = getattr(e, meth)(*a, **kw)
                    if t is not None:
                        ins.then_inc(t[0], step)
            return f

        block.sync(runner(self.q["sp"]))
        block.scalar(runner(self.q["act"]))
        block.vector(runner(self.q["dve"]))
        block.gpsimd(runner(self.q["pool"]))
        block.tensor(runner(self.q["pe"]))


class Ring:
    def __init__(self, S, tiles, dma=False):
        self.tiles = tiles
        self.free = [[] for _ in tiles]
        self.i = 0
        self.held = set()
        self.vs = [VSem(S, 16) for _ in tiles] if dma else None

    def get(self):
        idx = self.i % len(self.tiles)
        self.i += 1
        assert idx not in self.held, "ring buffer re-acquired before release"
        self.held.add(idx)
        w = self.free[idx]
        self.free[idx] = []
        return idx, self.tiles[idx], w

    def release(self, idx, tok):
        assert tok is not None
        self.held.discard(idx)
        self.free[idx].append(tok)


def build_program(cfg):
    nc = bass.Bass("TRN2", target_bir_lowering=False)
    NT, L, DEPTH = cfg.NT, cfg.L, cfg.depth
    GF, GM = cfg.GF, cfg.GM
    GX = max(GF, GM)
    NBLK = cfg.nblk

    def din(name, shape):
        return nc.dram_tensor(name, shape, F32, kind="ExternalInput").ap()

    xT = din("xT", [KC, 128, NT])
    outT = nc.dram_tensor("outT", [KC, 128, NT], F32, kind="ExternalOutput").ap()
    hT = nc.dram_tensor("hT", [KC, 128, NT], F32, kind="Internal").ap()
    w_gu = din("w_gu", [DEPTH * 2 * 2 * FC, 128, WT])
    w_dn = din("w_dn", [DEPTH * 2 * KC * 4, 128, 11 * 128])
    w_in = din("w_in", [DEPTH * 32, 128, WT])
    w_pl = din("w_pl", [DEPTH, 128, WT])
    w_ot = din("w_ot", [DEPTH * KC, 128, WT])
    cvec = din("cvec", [128, (3 * DEPTH + 1) * KC + DEPTH * 8])
    gsub = din("gsub", [128, DEPTH * 128])
    lamv = din("lamv", [128, DEPTH * 256])
    cmat = din("cmat", [128, 128 + 128 + 64 + 128])
    ropec = din("ropec", [128, L])
    ropes = din("ropes", [128, L])
    NCV = (3 * DEPTH + 1) * KC + DEPTH * 8

    es = ExitStack()
    S = Sched(nc, es)

    def sb(name, shape, dt):
        return es.enter_context(nc.sbuf_tensor(name, shape, dt))

    NBW = 8
    wbufs = [sb(f"wb{i}", [128, WT], BF16) for i in range(NBW)]
    cv = sb("cv", [128, NCV], F32)
    gsb = sb("gsb", [128, DEPTH * 128], F32)
    invc_t = sb("invc_t", [128, 64], F32)
    ones_bf = sb("ones_bf", [128, 128], BF16)
    ident = sb("ident", [128, 128], BF16)
    tri = sb("tri", [128, 128], BF16)
    rmat = sb("rmat", [128, 128], BF16)
    neglam = sb("neglam", [128, DEPTH], F32)
    lamtmp = sb("lamtmp", [128, 64], F32)
    lams = sb("lams", [128, 4], F32)

    xn = sb("xn", [128, KC, GX], BF16)
    rstd = sb("rstd", [128, GX], F32)
    hN = [sb(f"hN{i}", [128, GX], F32) for i in range(2)]
    hR = [sb(f"hR{i}", [128, GX], F32) for i in range(2)]
    ot = [sb(f"ot{i}", [128, GX], F32) for i in range(2)]
    sqf = sb("sqf", [128, GX], F32)
    acc = sb("acc", [128, GX], F32)
    hi_t = sb("hi_t", [128, GX], BF16)
    lo_t = sb("lo_t", [128, GX], BF16)
    FFN_BYTES = FC * GF * 2 + 2 * GF * 4
    vaug_n = NBLK * NHEAD * 129
    MIX_LAYOUT = [
        ("KT", NHEAD * L, BF16), ("VA", vaug_n, BF16), ("catT", KC * GM, BF16), ("QT", NHEAD * GM, BF16),
        ("cosg", GM, F32), ("sing", GM, F32), ("t1", GM, F32), ("t2", GM, F32), ("t3", GM, F32),
        ("qb0", GM, BF16), ("qb1", GM, BF16),
        ("vT0", GM, BF16), ("vT1", GM, BF16), ("ub", 16 + GM, F32), ("ua", 16 + GM, F32), ("uc", 16 + GM, F32),
        ("df", 2 * GM, BF16), ("uh", 8 * 16, F32),
        ("E0", 512, BF16), ("E1", 512, BF16), ("E2", 512, BF16), ("E3", 512, BF16),
        ("o0", 128, F32), ("o1", 128, F32), ("so", 128, F32), ("on0", 128, BF16), ("on1", 128, BF16),
        ("rc0", 4, F32), ("rc1", 4, F32),
    ]
    sz = {BF16: 2, F32: 4}
    MIX_BYTES = sum(((n * sz[dt] + 31) // 32) * 32 for _, n, dt in MIX_LAYOUT)
    UB = max(FFN_BYTES, MIX_BYTES) + 64
    uni = sb("uni", [128, UB // 4], F32)

    def carve(layout):
        out = {}
        off = 0
        for name, n, dt in layout:
            nb = ((n * sz[dt] + 31) // 32) * 32
            a = uni[:, off // 4:(off + nb) // 4]
            if dt == BF16:
                a = a.bitcast(BF16)
            out[name] = a[:, 0:n]
            off += nb
        return out

    FM = carve([("aT", FC * GF, BF16), ("sg0", GF, F32), ("sg1", GF, F32)])
    SM = carve([("lmv", DEPTH * 256, F32), ("cmf", 448, F32)])
    lmv, cmf = SM["lmv"], SM["cmf"]
    MM = carve(MIX_LAYOUT)
    aT = FM["aT"].rearrange("p (c g) -> p c g", c=FC)
    sgs = [FM["sg0"], FM["sg1"]]
    KT = MM["KT"].rearrange("p (h t) -> p h t", h=NHEAD)
    VA = MM["VA"].rearrange("p (b h e) -> p b h e", b=NBLK, h=NHEAD)
    catT = MM["catT"].rearrange("p (c g) -> p c g", c=KC)
    QT = MM["QT"].rearrange("p (h g) -> p h g", h=NHEAD)
    df = MM["df"].rearrange("p (i g) -> p i g", i=2)
    uh = MM["uh"].rearrange("p (c x) -> p c x", c=8)

    psb = [es.enter_context(nc.psum_tensor(f"ps{i}", [128, 512], F32)) for i in range(8)]

    wring = Ring(S, wbufs, dma=True)
    wstate = {"n": 0}

    def wreq(src, n):
        idx, buf, w = wring.get()
        t = S.dma("pool", lambda e, buf=buf, src=src, n=n: e.dma_start(out=buf[:, 0:n], in_=src),
                  wring.vs[idx], w)
        return idx, buf, t

    hNr = Ring(S, hN, dma=True)
    hRr = Ring(S, hR, dma=True)
    otr = Ring(S, ot, dma=True)
    misc_vs = VSem(S, 16)

    def sp_load(ring, src, n, extra=()):
        idx, buf, w = ring.get()
        t = S.dma("sp", lambda e, buf=buf, src=src, n=n: e.dma_start(out=buf[:, 0:n], in_=src),
                  ring.vs[idx], list(w) + list(extra))
        return idx, buf, t

    class PBanks:
        def __init__(self, ids):
            self.ids = ids
            self.free = {i: [] for i in ids}
            self.i = 0
            self.held = set()

        def get(self):
            b = self.ids[self.i % len(self.ids)]
            self.i += 1
            assert b not in self.held, f"PSUM bank {b} re-acquired before release"
            self.held.add(b)
            w = self.free[b]
            self.free[b] = []
            return b, w

        def release(self, b, tok):
            assert tok is not None
            self.held.discard(b)
            self.free[b].append(tok)

    def mm_group(out_ap, pairs, waits):
        n = len(pairs)
        t = None
        for i, (l, r) in enumerate(pairs):
            t = S.op("pe", lambda e, l=l, r=r, i=i: e.matmul(out_ap, l, r, start=(i == 0), stop=(i == n - 1)),
                     waits if i == 0 else (), tok=(i == n - 1))
        return t

    state = {"xn_rd": None, "last_store": []}

    t_cv = S.dma("sp", lambda e: e.dma_start(out=cv[:], in_=cvec), misc_vs)
    t_gs = S.dma("sp", lambda e: e.dma_start(out=gsb[:], in_=gsub), VSem(S, 16))
    t_lm = S.dma("sp", lambda e: e.dma_start(out=lmv[:], in_=lamv), VSem(S, 16))
    t_cm = S.dma("sp", lambda e: e.dma_start(out=cmf[:], in_=cmat), VSem(S, 16))
    t_one = S.op("dve", lambda e: e.memset(ones_bf[:], 1.0))
    t_id = S.op("dve", lambda e: e.tensor_copy(out=ident[:], in_=cmf[:, 0:128]), [t_cm])
    t_tri = S.op("dve", lambda e: e.tensor_copy(out=tri[:], in_=cmf[:, 128:256]), [t_cm])
    t_rm = S.op("dve", lambda e: e.tensor_copy(out=rmat[:], in_=cmf[:, 320:448]), [t_cm])
    t_inv = S.op("dve", lambda e: e.tensor_copy(out=invc_t[:], in_=cmf[:, 256:320]), [t_cm])
    invc = invc_t[:, :]
    t_setup = [t_cv, t_one, t_id, t_tri, t_inv, t_rm]
    tc = None
    for l in range(DEPTH):
        lam_init = 0.8 - 0.6 * math.exp(-0.3 * l)
        b = l * 256
        ta = S.op("dve", lambda e, b=b: e.tensor_tensor(out=lamtmp[:], in0=lmv[:, b:b + 64], in1=lmv[:, b + 64:b + 128], op=ALU.mult), [t_lm, tc])
        ta = S.op("dve", lambda e: e.reduce_sum(out=lams[:, 0:1], in_=lamtmp[:], axis=AX.X), [ta])
        tb = S.op("dve", lambda e, b=b: e.tensor_tensor(out=lamtmp[:], in0=lmv[:, b + 128:b + 192], in1=lmv[:, b + 192:b + 256], op=ALU.mult), [ta])
        tb = S.op("dve", lambda e: e.reduce_sum(out=lams[:, 1:2], in_=lamtmp[:], axis=AX.X), [tb])
        te = S.op("act", lambda e: e.activation(out=lams[:, 2:4], in_=lams[:, 0:2], func=AF.Exp), [tb])
        tc = S.op("dve", lambda e: e.tensor_tensor(out=lams[:, 0:1], in0=lams[:, 3:4], in1=lams[:, 2:3], op=ALU.subtract), [te])
        tc = S.op("dve", lambda e, l=l, li=lam_init: e.tensor_scalar(out=neglam[:, l:l + 1], in0=lams[:, 0:1], scalar1=-li, scalar2=None, op0=ALU.add), [tc])
        tg = S.op("dve", lambda e, l=l, li=lam_init: e.tensor_scalar(out=gsb[:, l * 128:(l + 1) * 128], in0=gsb[:, l * 128:(l + 1) * 128], scalar1=1.0 - li, scalar2=None, op0=ALU.mult), [t_gs])
        t_setup += [tc, tg]

    def barrier():
        toks = list(state["last_store"]) + list(t_setup)
        state["last_store"] = []
        for e in ("sp", "act", "dve", "pe"):
            S.op(e, (lambda en: (lambda eng: eng.nop()))(e), toks, tok=False)

    def norm_steps(src, c0, G, gcol, banks, final_dst=None, res=None):
        subs = split_sub(G)
        tacc = None
        for k in range(KC):
            i, buf, tl = sp_load(hNr, src[k, :, c0:c0 + G], G)
            if k == 0:
                tacc = S.op("act", lambda e, buf=buf: e.activation(out=acc[:, 0:G], in_=buf[:, 0:G], func=AF.Square), [tl, state.get("acc_rd")])
                hNr.release(i, tacc)
            else:
                ts = S.op("act", lambda e, buf=buf: e.activation(out=sqf[:, 0:G], in_=buf[:, 0:G], func=AF.Square), [tl, state.get("sq_rd")])
                hNr.release(i, ts)
                tacc = S.op("dve", lambda e: e.tensor_tensor(out=acc[:, 0:G], in0=acc[:, 0:G], in1=sqf[:, 0:G], op=ALU.add), [ts, tacc])
                state["sq_rd"] = tacc
            yield
        th = S.op("dve", lambda e: e.tensor_copy(out=hi_t[:, 0:G], in_=acc[:, 0:G]), [tacc, state.get("hilo_rd")])
        t2 = S.op("dve", lambda e: e.tensor_tensor(out=acc[:, 0:G], in0=acc[:, 0:G], in1=hi_t[:, 0:G], op=ALU.subtract), [th])
        tlo = S.op("dve", lambda e: e.tensor_copy(out=lo_t[:, 0:G], in_=acc[:, 0:G]), [t2])
        state["acc_rd"] = tlo
        tr = None
        for s, (off, n) in enumerate(subs):
            bk, w = banks.get()
            tp = mm_group(psb[bk][:, 0:n], [(ones_bf[:, :], hi_t[:, off:off + n]), (ones_bf[:, :], lo_t[:, off:off + n])], [th, tlo] + w)
            state["hilo_rd"] = tp
            t1 = S.op("act", lambda e, bk=bk, off=off, n=n: e.activation(out=rstd[:, off:off + n], in_=psb[bk][:, 0:n], func=AF.Ln, scale=1.0 / D, bias=eps_ap), [tp, state.get("rstd_rd")])
            banks.release(bk, t1)
            tr = S.op("act", lambda e, off=off, n=n: e.activation(out=rstd[:, off:off + n], in_=rstd[:, off:off + n], func=AF.Exp, scale=-0.5), [t1])
        yield
        outs = []
        for k in range(KC):
            i, buf, tl = sp_load(hNr, src[k, :, c0:c0 + G], G)
            if final_dst is None:
                tx = S.op("dve", lambda e, buf=buf, k=k: e.scalar_tensor_tensor(
                    out=xn[:, k, 0:G], in0=buf[:, 0:G], scalar=cv[:, gcol + k:gcol + k + 1], in1=rstd[:, 0:G],
                    op0=ALU.mult, op1=ALU.mult), [tl, tr, state["xn_rd"]])
                hNr.release(i, tx)
                outs.append(tx)
                state["rstd_rd"] = tx
            else:
                oi, obuf, ow = otr.get()
                tx = S.op("dve", lambda e, buf=buf, obuf=obuf, k=k: e.scalar_tensor_tensor(
                    out=obuf[:, 0:G], in0=buf[:, 0:G], scalar=cv[:, gcol + k:gcol + k + 1], in1=rstd[:, 0:G],
                    op0=ALU.mult, op1=ALU.mult), [tl, tr] + ow)
                hNr.release(i, tx)
                state["rstd_rd"] = tx
                tst = S.dma("sp", lambda e, obuf=obuf, k=k: e.dma_start(out=final_dst[k, :, c0:c0 + G], in_=obuf[:, 0:G]), otr.vs[oi], [tx])
                otr.release(oi, tst)
                state["last_store"].append(tst)
            yield
        if res is not None:
            res["xtok"] = outs

    def norm_pass(src, c0, G, gcol, banks, final_dst=None):
        res = {}
        for _ in norm_steps(src, c0, G, gcol, banks, final_dst, res):
            pass
        return res["xtok"]

    def drive(gen, st, j):
        if gen is None:
            return
        if j < 6:
            target = min(16, 3 * (j + 1))
        elif j < 10:
            target = 16
        elif j == 10:
            target = 17
        else:
            target = 17 + 4 * (j - 10)
        while st["n"] < target:
            try:
                next(gen)
            except StopIteration:
                st["n"] = 10 ** 6
                return
            st["n"] += 1

    def upto(gen, st, target):
        if gen is None:
            return
        while st["n"] < target:
            try:
                next(gen)
            except StopIteration:
                st["n"] = 10 ** 6
                return
            st["n"] += 1

    def advance(gen, n):
        if gen is None:
            return
        for _ in range(n):
            try:
                next(gen)
            except StopIteration:
                return

    eps_t = sb("eps_t", [128, 2], F32)
    S.op("dve", lambda e: e.memset(eps_t[:, 0:1], NORM_EPS), tok=False)
    t_eps = S.op("dve", lambda e: e.memset(eps_t[:, 1:2], SUBLN_EPS))
    t_setup.append(t_eps)
    eps_ap = eps_t[:, 0:1]
    eps2_ap = eps_t[:, 1:2]

    def ffn_phase(l, f, src, dst):
        banks = PBanks(list(range(8)))
        gcol = (l * 3 + (0 if f == 0 else 2)) * KC
        gu_base = ((l * 2 + f) * 2) * FC
        dn_base = (l * 2 + f) * KC * 4
        sgr = Ring(S, sgs)
        groups = cfg.ffn_groups
        nres = {}
        advance(norm_steps(src, groups[0][0], groups[0][1], gcol, banks, None, nres), 1000)
        for gi_, (c0, G) in enumerate(groups):
            subs = split_sub(G)
            xtok = nres["xtok"]
            a_tok = []
            last_gu = None
            ngen = None
            nst = {"n": 0}
            if gi_ + 1 < len(groups):
                nres = {}
                ngen = norm_steps(src, groups[gi_ + 1][0], groups[gi_ + 1][1], gcol, banks, None, nres)
            for c in range(FC):
                if c >= FC - 16:
                    upto(ngen, nst, c - (FC - 16) + 1)
                gi, gb, tg = wreq(w_gu[gu_base + c], WT)
                ui, ubf, tu = wreq(w_gu[gu_base + FC + c], WT)
                gbk = []
                ubk = []
                for wb, tw, lst in ((gb, tg, gbk), (ubf, tu, ubk)):
                    for s, (off, n) in enumerate(subs):
                        bk, w = banks.get()
                        t = mm_group(psb[bk][:, 0:n],
                                     [(wb[:, k * 128:(k + 1) * 128], xn[:, k, off:off + n]) for k in range(KC)],
                                     [tw] + w + (xtok if c == 0 else []))
                        lst.append((bk, t))
                last_gu = ubk[-1][1]
                wring.release(gi, gbk[-1][1])
                wring.release(ui, last_gu)
                si, sgb, sw = sgr.get()
                for s, (off, n) in enumerate(subs):
                    bk, t = gbk[s]
                    t1 = S.op("act", lambda e, sgb=sgb, bk=bk, off=off, n=n: e.activation(out=sgb[:, off:off + n], in_=psb[bk][:, 0:n], func=AF.Silu), [t] + sw)
                    banks.release(bk, t1)
                    bk2, t2 = ubk[s]
                    t3 = S.op("dve", lambda e, sgb=sgb, bk2=bk2, off=off, n=n, c=c: e.tensor_tensor(out=aT[:, c, off:off + n], in0=psb[bk2][:, 0:n], in1=sgb[:, off:off + n], op=ALU.mult), [t1, t2])
                    banks.release(bk2, t3)
                sgr.release(si, t3)
                a_tok.append(t3)
            state["xn_rd"] = last_gu
            pre = {}
            for j in range(min(2, KC)):
                pre[j] = sp_load(hRr, src[j, :, c0:c0 + G], G)
            for j in range(KC):
                tiles = []
                for qd in range(4):
                    tiles.append(wreq(w_dn[dn_base + j * 4 + qd], 11 * 128))
                bks = []
                for s, (off, n) in enumerate(subs):
                    bk, w = banks.get()
                    pairs = []
                    for c in range(FC):
                        wb = tiles[c // 11][1]
                        cc = c % 11
                        pairs.append((wb[:, cc * 128:(cc + 1) * 128], aT[:, c, off:off + n]))
                    t = mm_group(psb[bk][:, 0:n], pairs, [tt[2] for tt in tiles] + w + (a_tok if j == 0 else []))
                    bks.append((bk, t))
                for tt in tiles:
                    wring.release(tt[0], bks[-1][1])
                ri, rbuf, rt = pre.pop(j)
                oi, obuf, ow = otr.get()
                for s, (off, n) in enumerate(subs):
                    bk, t = bks[s]
                    te = S.op("dve", lambda e, obuf=obuf, rbuf=rbuf, bk=bk, off=off, n=n: e.scalar_tensor_tensor(
                        out=obuf[:, off:off + n], in0=psb[bk][:, 0:n], scalar=0.5, in1=rbuf[:, off:off + n],
                        op0=ALU.mult, op1=ALU.add), [t, rt] + ow)
                    banks.release(bk, te)
                hRr.release(ri, te)
                if j + 2 < KC:
                    pre[j + 2] = sp_load(hRr, src[j + 2, :, c0:c0 + G], G)
                tst = S.dma("sp", lambda e, obuf=obuf, j=j: e.dma_start(out=dst[j, :, c0:c0 + G], in_=obuf[:, 0:G]), otr.vs[oi], [te])
                otr.release(oi, tst)
                state["last_store"].append(tst)
                if j == 2:
                    upto(ngen, nst, 17)
                elif j >= 3:
                    upto(ngen, nst, 17 + 2 * (j - 2))
            advance(ngen, 1000)

    Er = Ring(S, [MM["E0"], MM["E1"], MM["E2"], MM["E3"]])
    o_r = Ring(S, [MM["o0"], MM["o1"]])
    on_r = Ring(S, [MM["on0"], MM["on1"]])
    rc_r = Ring(S, [MM["rc0"], MM["rc1"]])
    vT_r = Ring(S, [MM["vT0"], MM["vT1"]])

    def mixer_phase(l):
        pbanks = PBanks([0, 1, 2])
        sbanks = PBanks([3, 4, 5])
        obanks = PBanks([6, 7])
        qb_r = Ring(S, [MM["qb0"], MM["qb1"]])
        gcol = (l * 3 + 1) * KC
        pscol = (3 * DEPTH + 1) * KC + l * 8
        cos_vs, sin_vs = VSem(S, 16), VSem(S, 16)
        wl_vs = VSem(S, 16)
        rope_last = [None]
        t_va = None
        allg = [(s_, t_, g_) for s_ in range(cfg.nseq) for (t_, g_) in cfg.mix_groups]
        nres = {}
        advance(norm_steps(hT, allg[0][0] * L + allg[0][1], allg[0][2], gcol, pbanks, None, nres), 1000)
        pend_t = []

        def flush_t():
            while pend_t:
                pend_t.pop(0)()

        for sq_i in range(cfg.nseq):
            t_va = S.op("dve", lambda e: e.memset(MM["VA"], 1.0))
            t_uh = S.op("dve", lambda e: e.memset(MM["uh"], 0.0))
            for gidx, (t0, G) in enumerate(cfg.mix_groups):
                c0 = sq_i * L + t0
                gflat = sq_i * len(cfg.mix_groups) + gidx
                subs = split_sub(G)
                blocks = []
                o = 0
                while o < G:
                    nq = min(128, G - o)
                    blocks.append(((t0 + o) // 128, o, nq))
                    o += nq
                xtok = nres["xtok"]
                tcs = S.dma("sp", lambda e, t0=t0, G=G: e.dma_start(out=MM["cosg"][:, 0:G], in_=ropec[:, t0:t0 + G]), cos_vs, [rope_last[0]])
                tsn = S.dma("sp", lambda e, t0=t0, G=G: e.dma_start(out=MM["sing"][:, 0:G], in_=ropes[:, t0:t0 + G]), sin_vs, [rope_last[0]])
                first_proj = [True]

                def proj(widx):
                    wi, wb, tw = wreq(w_in[l * 32 + widx], WT)
                    res = []
                    for s, (off, n) in enumerate(subs):
                        bk, w = pbanks.get()
                        t = mm_group(psb[bk][:, 0:n],
                                     [(wb[:, k * 128:(k + 1) * 128], xn[:, k, off:off + n]) for k in range(KC)],
                                     [tw] + w + (xtok if first_proj[0] else []))
                        first_proj[0] = False
                        res.append((bk, t))
                    wring.release(wi, res[-1][1])
                    state["xn_rd"] = res[-1][1]
                    return res

                def rope1(pa, t1buf):
                    qi, qbuf, qw = qb_r.get()
                    ta_ = None
                    for s, (off, n) in enumerate(subs):
                        b1, ta = pa[s]
                        tc_ = S.op("act", lambda e, b1=b1, off=off, n=n: e.activation(out=qbuf[:, off:off + n], in_=psb[b1][:, 0:n], func=AF.Copy), [ta] + qw)
                        ta_ = S.op("dve", lambda e, b1=b1, off=off, n=n: e.tensor_tensor(out=t1buf[:, off:off + n], in0=psb[b1][:, 0:n], in1=MM["cosg"][:, off:off + n], op=ALU.mult), [tc_, tcs, rope_last[0]])
                        pbanks.release(b1, ta_)
                    return qi, qbuf, tc_, ta_

                def rope2(r1, t1buf, dst_fn):
                    qi, qbuf, tc_, ta_ = r1
                    tl = None
                    tr_ = None
                    for s, (off, n) in enumerate(subs):
                        bk, w = pbanks.get()
                        tr_ = S.op("pe", lambda e, bk=bk, off=off, n=n: e.matmul(psb[bk][:, 0:n], rmat[:, :], qbuf[:, off:off + n], start=True, stop=True), [tc_] + w)
                        x2 = S.op("dve", lambda e, bk=bk, off=off, n=n: e.tensor_tensor(out=MM["t3"][:, off:off + n], in0=psb[bk][:, 0:n], in1=MM["sing"][:, off:off + n], op=ALU.mult), [tr_, tsn, rope_last[0]])
                        pbanks.release(bk, x2)
                        tl = S.op("dve", lambda e, off=off, n=n: e.tensor_tensor(out=dst_fn(off, n), in0=t1buf[:, off:off + n], in1=MM["t3"][:, off:off + n], op=ALU.add), [x2, ta_])
                        rope_last[0] = tl
                    qb_r.release(qi, tr_)
                    return tl

                def pool_gen():
                    cat_tok = state["cat_tok"]
                    ub, ua, uc = MM["ub"], MM["ua"], MM["uc"]
                    pool_last = state.get("pool_last")
                    for g in range(4):
                        w = WINDOWS[g]
                        dtoks = []
                        for ic in range(2):
                            ucx = 2 * g + ic
                            pu = proj(24 + ucx)
                            th = S.op("dve", lambda e, ucx=ucx: e.tensor_copy(out=ub[:, 0:16], in_=uh[:, ucx, :]), [t_uh, pool_last])
                            tu = None
                            for s, (off, n) in enumerate(subs):
                                bk, t = pu[s]
                                tu = S.op("act", lambda e, bk=bk, off=off, n=n: e.activation(out=ub[:, 16 + off:16 + off + n], in_=psb[bk][:, 0:n], func=AF.Copy), [t, pool_last])
                                pbanks.release(bk, tu)
                            th2 = S.op("dve", lambda e, ucx=ucx, G=G: e.tensor_copy(out=uh[:, ucx, :], in_=ub[:, G:G + 16]), [tu, th])
                            srcb, lvl, tprev = ub, 1, th2
                            tmps = [ua, uc]
                            ti = 0
                            while lvl < w:
                                dstb = tmps[ti % 2]
                                ti += 1
                                lo = 2 * lvl - 1
                                tprev = S.op("dve", lambda e, srcb=srcb, dstb=dstb, lo=lo, lvl=lvl, G=G: e.tensor_tensor(
                                    out=dstb[:, lo:16 + G], in0=srcb[:, lo:16 + G], in1=srcb[:, lo - lvl:16 + G - lvl], op=ALU.add), [tprev])
                                srcb = dstb
                                lvl *= 2
                            td = S.op("dve", lambda e, srcb=srcb, ic=ic, w=w, G=G: e.scalar_tensor_tensor(
                                out=df[:, ic, 0:G], in0=srcb[:, 16:16 + G], scalar=1.0 / w, in1=ub[:, 16:16 + G],
                                op0=ALU.mult, op1=ALU.subtract), [tprev, state.get("df_rd")])
                            if t0 == 0:
                                other = tmps[ti % 2]
                                tf = S.op("dve", lambda e, srcb=srcb, other=other, g=g: e.tensor_tensor(out=other[:, 0:16], in0=srcb[:, 16:32], in1=invc[:, g * 16:(g + 1) * 16], op=ALU.mult), [td])
                                td = S.op("dve", lambda e, other=other, ic=ic: e.tensor_tensor(out=df[:, ic, 0:16], in0=other[:, 0:16], in1=ub[:, 16:32], op=ALU.subtract), [tf])
                            pool_last = td
                            dtoks.append(td)
                            yield
                        wi_p, wb_p, tw_p = wreq(w_pl[l], WT)
                        for oc in range(2):
                            res = []
                            for s, (off, n) in enumerate(subs):
                                bk, wv = pbanks.get()
                                pairs = []
                                for ic in range(2):
                                    col = ((g * 2 + ic) * 2 + oc) * 128
                                    pairs.append((wb_p[:, col:col + 128], df[:, ic, off:off + n]))
                                t = mm_group(psb[bk][:, 0:n], pairs, [tw_p] + wv + dtoks)
                                res.append((bk, t))
                            state["df_rd"] = res[-1][1]
                            cidx = 8 + 2 * g + oc
                            for s, (off, n) in enumerate(subs):
                                bk, t = res[s]
                                tcp = S.op("dve", lambda e, bk=bk, off=off, n=n, cidx=cidx: e.tensor_scalar(
                                    out=catT[:, cidx, off:off + n], in0=psb[bk][:, 0:n], scalar1=cv[:, pscol + cidx - 8:pscol + cidx - 7],
                                    scalar2=None, op0=ALU.mult), [t, state["cat_rd"]])
                                pbanks.release(bk, tcp)
                                cat_tok.append(tcp)
                        wring.release(wi_p, state["df_rd"])
                        yield
                    state["pool_last"] = pool_last
                pgen = pool_gen()
                ngen = None
                nst = {"n": 0}
                nres_next = None
                for h in range(NHEAD):
                    pq = proj(h)
                    r1q = rope1(pq, MM["t1"])
                    pk = proj(8 + h)
                    r1k = rope1(pk, MM["t2"])
                    tq = rope2(r1q, MM["t1"], lambda off, n, h=h: QT[:, h, off:off + n])
                    tk = rope2(r1k, MM["t2"], lambda off, n, h=h: KT[:, h, t0 + off:t0 + off + n])
                    flush_t()
                    pv = proj(16 + h)
                    vi, vbuf, vw = vT_r.get()
                    tv = None
                    for s, (off, n) in enumerate(subs):
                        bk, t = pv[s]
                        tv = S.op("act", lambda e, vbuf=vbuf, bk=bk, off=off, n=n: e.activation(out=vbuf[:, off:off + n], in_=psb[bk][:, 0:n], func=AF.Copy), [t] + vw)
                        pbanks.release(bk, tv)
                    tvl = []
                    tlast = None
                    tb_id, w = pbanks.get()
                    tbank = psb[tb_id][:, :].bitcast(BF16)
                    for sl, (gb, o, nq) in enumerate(blocks):
                        tlast = S.op("pe", lambda e, sl=sl, vbuf=vbuf, o=o, nq=nq: e.transpose(tbank[0:nq, sl * 128:(sl + 1) * 128], vbuf[:, o:o + nq], ident[:, :]), [tv] + w)
                    for sl, (gb, o, nq) in enumerate(blocks):
                        tcp = S.op("act", lambda e, sl=sl, gb=gb, nq=nq, h=h: e.activation(out=VA[0:nq, gb, h, 0:128], in_=tbank[0:nq, sl * 128:(sl + 1) * 128], func=AF.Copy), [tlast, t_va])
                        tvl.append(tcp)
                    pbanks.release(tb_id, tcp)
                    vT_r.release(vi, tlast)
                    chunks = []
                    for (gb, o, nq) in blocks:
                        for m in range(2):
                            kbs = list(range(gb + 1))
                            for ci in range(0, len(kbs), 4):
                                ch = kbs[ci:ci + 4]
                                if len(ch) > 1 and (ch[-1] + 1) * 128 > L:
                                    chunks.append((gb, o, nq, m, ch[:-1]))
                                    chunks.append((gb, o, nq, m, ch[-1:]))
                                else:
                                    chunks.append((gb, o, nq, m, ch))
                    cur_o = {}
                    sinfo = [None] * len(chunks)

                    def emit_S(ci):
                        gb, o, nq, m, ch = chunks[ci]
                        nk = min(128, L - ch[0] * 128)
                        bk, w = sbanks.get()
                        t = None
                        for i, kb in enumerate(ch):
                            t = S.op("pe", lambda e, bk=bk, i=i, kb=kb, nk=nk, nq=nq, m=m, o=o: e.matmul(
                                psb[bk][0:nk, i * nq:(i + 1) * nq],
                                KT[m * 64:(m + 1) * 64, h, kb * 128:kb * 128 + nk],
                                QT[m * 64:(m + 1) * 64, h, o:o + nq], start=True, stop=True),
                                ([tq, tk] + w) if i == 0 else (), tok=(i == len(ch) - 1))
                        ei, eb, ew = Er.get()
                        ncol = len(ch) * nq
                        te = S.op("act", lambda e, eb=eb, bk=bk, nk=nk, ncol=ncol: e.activation(out=eb[0:nk, 0:ncol], in_=psb[bk][0:nk, 0:ncol], func=AF.Exp, scale=0.125), [t] + ew)
                        sbanks.release(bk, te)
                        if ch[-1] == gb:
                            i = len(ch) - 1
                            te = S.op("dve", lambda e, eb=eb, i=i, nk=nk, nq=nq: e.tensor_tensor(out=eb[0:nk, i * nq:(i + 1) * nq], in0=eb[0:nk, i * nq:(i + 1) * nq], in1=tri[0:nk, 0:nq], op=ALU.mult), [te])
                        sinfo[ci] = (ei, eb, te, nk)

                    def emit_AV(ci):
                        gb, o, nq, m, ch = chunks[ci]
                        ei, eb, te, nk = sinfo[ci]
                        if (gb, m) == (gb, 0) and ch[0] == 0 and m == 0:
                            ob, w = obanks.get()
                            cur_o[gb] = (ob, w)
                        ob, w0 = cur_o[gb]
                        t = None
                        for i, kb in enumerate(ch):
                            w = [te] + tvl
                            if kb == 0:
                                w = w + w0
                            t = S.op("pe", lambda e, ob=ob, eb=eb, i=i, kb=kb, nk=nk, nq=nq, m=m, gb=gb: e.matmul(
                                psb[ob][0:nq, m * 256:m * 256 + 129],
                                eb[0:nk, i * nq:(i + 1) * nq],
                                VA[0:nk, kb, h, 0:129], start=(kb == 0), stop=(kb == gb)),
                                w if i == 0 or kb == 0 else (), tok=(i == len(ch) - 1))
                        Er.release(ei, t)
                        if m == 1 and ch[-1] == gb:
                            combine(gb, o, nq, ob, t)

                    def combine(gb, o, nq, ob, tav):
                        P = psb[ob]
                        ri, rc, rw = rc_r.get()
                        a1 = S.op("dve", lambda e: e.reciprocal(out=rc[0:nq, 0:1], in_=P[0:nq, 128:129]), [tav] + rw)
                        a2 = S.op("dve", lambda e: e.reciprocal(out=rc[0:nq, 1:2], in_=P[0:nq, 384:385]), [tav])
                        a3 = S.op("dve", lambda e: e.tensor_tensor(out=rc[0:nq, 1:2], in0=rc[0:nq, 1:2], in1=neglam[0:nq, l:l + 1], op=ALU.mult), [a2])
                        oi, ob_, ow = o_r.get()
                        a4 = S.op("dve", lambda e: e.tensor_scalar(out=ob_[0:nq, 0:128], in0=P[0:nq, 0:128], scalar1=rc[0:nq, 0:1], scalar2=None, op0=ALU.mult), [a1] + ow)
                        a5 = S.op("dve", lambda e: e.scalar_tensor_tensor(out=ob_[0:nq, 0:128], in0=P[0:nq, 256:384], scalar=rc[0:nq, 1:2], in1=ob_[0:nq, 0:128], op0=ALU.mult, op1=ALU.add), [a3, a4])
                        obanks.release(ob, a5)
                        a6 = S.op("dve", lambda e: e.tensor_tensor(out=MM["so"][0:nq, 0:128], in0=ob_[0:nq, 0:128], in1=ob_[0:nq, 0:128], op=ALU.mult), [a5, state.get("so_rd")])
                        a7 = S.op("dve", lambda e: e.reduce_sum(out=rc[0:nq, 2:3], in_=MM["so"][0:nq, 0:128], axis=AX.X), [a6])
                        state["so_rd"] = a7
                        a8 = S.op("act", lambda e: e.activation(out=rc[0:nq, 3:4], in_=rc[0:nq, 2:3], func=AF.Ln, scale=1.0 / 128, bias=eps2_ap[0:nq, :]), [a7])
                        a9 = S.op("act", lambda e: e.activation(out=rc[0:nq, 3:4], in_=rc[0:nq, 3:4], func=AF.Exp, scale=-0.5), [a8])
                        ni, nb, nw = on_r.get()
                        a10 = S.op("dve", lambda e: e.scalar_tensor_tensor(out=nb[0:nq, 0:128], in0=ob_[0:nq, 0:128], scalar=rc[0:nq, 3:4], in1=gsb[0:nq, l * 128:(l + 1) * 128], op0=ALU.mult, op1=ALU.mult), [a9] + nw)
                        o_r.release(oi, a10)
                        rc_r.release(ri, a10)
                        def later(h=h, o=o, nq=nq, nb=nb, ni=ni, a10=a10, cat_tok=cat_tok):
                            tb_id, w = pbanks.get()
                            tbank = psb[tb_id][:, :].bitcast(BF16)
                            a11 = S.op("pe", lambda e: e.transpose(tbank[0:128, 0:nq], nb[0:nq, 0:128], ident[0:nq, 0:nq]), [a10] + w)
                            on_r.release(ni, a11)
                            a12 = S.op("act", lambda e: e.activation(out=catT[:, h, o:o + nq], in_=tbank[0:128, 0:nq], func=AF.Copy), [a11, state["cat_rd"]])
                            pbanks.release(tb_id, a12)
                            cat_tok.append(a12)
                        flush_t()
                        pend_t.append(later)

                    if h == 0:
                        cat_tok = []
                        state["cat_tok"] = cat_tok
                    else:
                        cat_tok = state["cat_tok"]
                    emit_S(0)
                    if len(chunks) > 1:
                        emit_S(1)
                    for ci in range(len(chunks)):
                        if ci + 2 < len(chunks):
                            emit_S(ci + 2)
                        emit_AV(ci)
                    advance(pgen, 1 if h % 2 == 0 else 2)
                    if h == 5 and gflat + 1 < len(allg):
                        nres_next = {}
                        ns_, nt_, ng_ = allg[gflat + 1]
                        ngen = norm_steps(hT, ns_ * L + nt_, ng_, gcol, pbanks, None, nres_next)
                    if h >= 6:
                        upto(ngen, nst, 8 * (h - 5))

                flush_t()
                advance(pgen, 1000)
                upto(ngen, nst, 17)
                pre = {}
                for j in range(2):
                    pre[j] = sp_load(hRr, hT[j, :, c0:c0 + G], G)
                last_o = None
                for j in range(KC):
                    wi, wb, tw = wreq(w_ot[l * KC + j], WT)
                    bks = []
                    for s, (off, n) in enumerate(subs):
                        bk, wv = pbanks.get()
                        t = mm_group(psb[bk][:, 0:n],
                                     [(wb[:, c * 128:(c + 1) * 128], catT[:, c, off:off + n]) for c in range(KC)],
                                     [tw] + wv + (cat_tok if j == 0 else []))
                        bks.append((bk, t))
                    wring.release(wi, bks[-1][1])
                    last_o = bks[-1][1]
                    ri, rbuf, rt = pre.pop(j)
                    oi, obuf, ow = otr.get()
                    te = None
                    for s, (off, n) in enumerate(subs):
                        bk, t = bks[s]
                        te = S.op("dve", lambda e, obuf=obuf, rbuf=rbuf, bk=bk, off=off, n=n: e.tensor_tensor(
                            out=obuf[:, off:off + n], in0=psb[bk][:, 0:n], in1=rbuf[:, off:off + n], op=ALU.add), [t, rt] + ow)
                        pbanks.release(bk, te)
                    hRr.release(ri, te)
                    if j + 2 < KC:
                        pre[j + 2] = sp_load(hRr, hT[j + 2, :, c0:c0 + G], G)
                    tst = S.dma("sp", lambda e, obuf=obuf, j=j, c0=c0, G=G: e.dma_start(out=hT[j, :, c0:c0 + G], in_=obuf[:, 0:G]), otr.vs[oi], [te])
                    otr.release(oi, tst)
                    state["last_store"].append(tst)
                    upto(ngen, nst, 17 + 2 * (j + 1))
                advance(ngen, 1000)
                if nres_next is not None:
                    nres = nres_next
                state["cat_rd"] = last_o

    state["cat_rd"] = None

    barrier()
    for l in range(DEPTH):
        ffn_phase(l, 0, xT if l == 0 else hT, hT)
        barrier()
        mixer_phase(l)
        barrier()
        ffn_phase(l, 1, hT, hT)
        barrier()
    fb = PBanks(list(range(8)))
    for (c0, G) in cfg.ffn_groups:
        norm_pass(hT, c0, G, 3 * DEPTH * KC, fb, final_dst=outT)
    S.op("sp", lambda e: e.nop(), state["last_store"], tok=False)

    import os
    if os.environ.get("K_DEBUG"):
        print("nsem", S.nsem, {k: len(v) for k, v in S.q.items()}, {k: (v.cnt) for k, v in S.vs.items()}, flush=True)
    with nc.Block() as block:
        S.emit(block)
    es.close()
    return nc


def _tile_w(W, nk, no):
    return np.ascontiguousarray(W.reshape(nk, 128, no, 128).transpose(2, 1, 0, 3)).reshape(no, 128, nk * 128)


def prep_weights(inp, depth):
    gu = np.empty((depth, 2, 2, FC, 128, WT), np.float32)
    dn = np.empty((depth, 2, KC, 4, 128, 11 * 128), np.float32)
    win = np.empty((depth, 32, 128, WT), np.float32)
    wpl = np.empty((depth, 128, WT), np.float32)
    wot = np.empty((depth, KC, 128, WT), np.float32)
    perm = np.arange(1024).reshape(16, 64)
    perm = np.concatenate([perm[:, 32:], perm[:, :32]], axis=1).reshape(-1)
    for l in range(depth):
        for f, pre in enumerate(("ffn1", "ffn2")):
            gu[l, f, 0] = _tile_w(np.asarray(inp[pre + "_w_gate"][l]), KC, FC)
            gu[l, f, 1] = _tile_w(np.asarray(inp[pre + "_w_up"][l]), KC, FC)
            d = _tile_w(np.asarray(inp[pre + "_w_down"][l]), FC, KC)
            dn[l, f] = d.reshape(KC, 128, 4, 11 * 128).transpose(0, 2, 1, 3)
        W = np.asarray(inp["w_in"][l])
        q, k, v, u = W[:, :1024], W[:, 1024:2048], W[:, 2048:3072], W[:, 3072:]
        win[l] = _tile_w(W, KC, 32)
        wp = np.asarray(inp["w_pool"][l])
        wpl[l] = wp.reshape(4, 2, 128, 2, 128).transpose(2, 0, 1, 3, 4).reshape(128, WT)
        wot[l] = _tile_w(np.asarray(inp["w_out"][l]), KC, KC)
    return {
        "w_gu": gu.reshape(-1, 128, WT), "w_dn": dn.reshape(-1, 128, 11 * 128),
        "w_in": win.reshape(-1, 128, WT), "w_pl": wpl, "w_ot": wot.reshape(-1, 128, WT),
    }


def prep_consts(inp, depth, L):
    def col(v):
        return np.asarray(v, np.float32).reshape(-1, 128).T
    cols = []
    for l in range(depth):
        cols += [col(inp["ffn1_norm_g"][l]), col(inp["mix_norm_g"][l]), col(inp["ffn2_norm_g"][l])]
    cols.append(col(inp["final_norm_g"]))
    for l in range(depth):
        cols.append(col(inp["pool_scale"][l]))
    cvec = np.ascontiguousarray(np.concatenate(cols, axis=1), np.float32)
    gsub = np.ascontiguousarray(np.broadcast_to(np.asarray(inp["subln_g"], np.float32)[:depth].reshape(1, -1), (128, depth * 128)))
    lam = np.stack([np.asarray(inp[k], np.float32)[:depth] for k in ("lam_q1", "lam_k1", "lam_q2", "lam_k2")], axis=1)
    lamv = np.ascontiguousarray(np.broadcast_to(lam.reshape(1, -1), (128, depth * 256)))
    cmat = np.zeros((128, 448), np.float32)
    cmat[:, 0:128] = np.eye(128, dtype=np.float32)
    kk = np.arange(128)
    cmat[:, 128:256] = (kk[None, :] >= kk[:, None]).astype(np.float32)
    for po in range(128):
        d = po % 64
        if d < 32:
            cmat[po + 32, 320 + po] = -1.0
        else:
            cmat[po - 32, 320 + po] = 1.0
    for g, w in enumerate(WINDOWS):
        t = np.arange(16)
        cmat[:, 256 + g * 16:256 + (g + 1) * 16] = (1.0 / np.minimum(t + 1, w)).astype(np.float32)[None, :]
    pos = np.arange(L, dtype=np.float32)
    inv_freq = (np.float32(1.0) / (np.float32(10000.0) ** (np.arange(0, 64, 2, dtype=np.float32) / np.float32(64)))).astype(np.float32)
    ang = (pos[:, None] * inv_freq[None, :]).astype(np.float32)
    c = np.cos(ang.astype(np.float64)).astype(np.float32).T
    s = np.sin(ang.astype(np.float64)).astype(np.float32).T
    ropec = np.ascontiguousarray(np.concatenate([c, c, c, c], axis=0))
    ropes = np.ascontiguousarray(np.concatenate([s, s, s, s], axis=0))
    return {"cvec": cvec, "gsub": gsub, "lamv": lamv, "cmat": cmat, "ropec": ropec, "ropes": ropes}


def run(inputs, cfg, ncores, trace=False):
    x = np.asarray(inputs["x"], np.float32)
    meta = np.asarray(inputs["meta_tokens"], np.float32)
    B, SEQ, _ = x.shape
    assert B == ncores * cfg.nseq and SEQ + NMETA == cfg.L
    shared = {}
    shared.update(prep_weights(inputs, cfg.depth))
    shared.update(prep_consts(inputs, cfg.depth, cfg.L))
    in_maps = []
    for c in range(ncores):
        cols = []
        for s in range(cfg.nseq):
            hb = np.concatenate([meta, x[c * cfg.nseq + s]], axis=0)
            cols.append(hb.T)
        xt = np.ascontiguousarray(np.concatenate(cols, axis=1)).reshape(KC, 128, cfg.NT)
        m = dict(shared)
        m["xT"] = xt
        in_maps.append(m)
    nc = build_program(cfg)
    res = run_bass_kernel_spmd(nc, in_maps, core_ids=list(range(ncores)), trace=trace)
    out = np.empty((B, SEQ, D), np.float32)
    for c in range(ncores):
        o = np.asarray(res.results[c]["outT"]).reshape(D, cfg.NT)
        for s in range(cfg.nseq):
            out[c * cfg.nseq + s] = o[:, s * cfg.L + NMETA:(s + 1) * cfg.L].T
    return out, res


def kernel(**inputs):
    cfg = Cfg(depth=4, nseq=2, seqlen=2048, ffn_g=688)
    out, _ = run(inputs, cfg, 8)
    return out
```
